# Optimizing a Trainium2 kernel written in Bass

```python
import jax, jax.numpy as jnp
from jax import lax
import numpy as np

D_MODEL = 1024
BATCH = 2
SEQ = 8192
DEPTH = 2

CHUNK = 64
CONV_W = 4
LRU_WIDTH = D_MODEL
LRU_BLOCKS = 16
LRU_BLOCK_DIM = LRU_WIDTH // LRU_BLOCKS
LRU_C = 8.0
SSD_EXPAND = 2
SSD_INNER = SSD_EXPAND * D_MODEL
SSD_HEAD_DIM = 64
SSD_HEADS = SSD_INNER // SSD_HEAD_DIM
SSD_GROUPS = 4
SSD_HEADS_PER_GROUP = SSD_HEADS // SSD_GROUPS
SSD_STATE = 128
SSD_CONV_DIM = SSD_INNER + 2 * SSD_GROUPS * SSD_STATE
N_BRANCH = 2
D_FF = ((8 * D_MODEL + 3 * 256 - 1) // (3 * 256)) * 256
EPS = 1e-6
IN_WIDTHS = (LRU_WIDTH, LRU_WIDTH, SSD_INNER, SSD_CONV_DIM, SSD_HEADS, N_BRANCH * D_MODEL)
IN_DIM = sum(IN_WIDTHS)

kernel_name = "hybrid_rglru_ssd_parallel_gated_block"


def _split(t, widths):
    offs = np.cumsum(widths)[:-1].tolist()
    return jnp.split(t, offs, axis=-1)


def rms_norm(x, g):
    xf = x.astype(jnp.float32)
    y = xf * lax.rsqrt(jnp.mean(xf * xf, axis=-1, keepdims=True) + EPS)
    return (y * g.astype(jnp.float32)).astype(x.dtype)


def causal_dw_conv(x, w, b):
    y = lax.conv_general_dilated(
        x, w[:, None, :].astype(x.dtype), window_strides=(1,), padding=[(CONV_W - 1, 0)],
        dimension_numbers=('NWC', 'WIO', 'NWC'), feature_group_count=x.shape[-1])
    return y + b.astype(x.dtype)


def rg_lru(x, w_a, b_a, w_x, b_x, lam):
    bsz, s, w = x.shape
    f32 = jnp.float32
    xf = x.astype(f32)
    xb = xf.reshape(bsz, s, LRU_BLOCKS, LRU_BLOCK_DIM)
    r = jax.nn.sigmoid(jnp.einsum('bshi,hij->bshj', xb, w_a.astype(f32)).reshape(bsz, s, w) + b_a.astype(f32))
    i = jax.nn.sigmoid(jnp.einsum('bshi,hij->bshj', xb, w_x.astype(f32)).reshape(bsz, s, w) + b_x.astype(f32))
    log_a = -LRU_C * r * jax.nn.softplus(-lam.astype(f32))
    a = jnp.exp(log_a)
    u = jnp.sqrt(-jnp.expm1(2.0 * log_a)) * (i * xf)

    def combine(lhs, rhs):
        a1, b1 = lhs
        a2, b2 = rhs
        return a1 * a2, a2 * b1 + b2

    _, h = lax.associative_scan(combine, (a, u), axis=1)
    return h.astype(x.dtype)


def ssd_scan(x, dt, A, Bm, Cm):
    b, s = x.shape[:2]
    c = s // CHUNK
    G, K, P, N = SSD_GROUPS, SSD_HEADS_PER_GROUP, SSD_HEAD_DIM, SSD_STATE
    xdt = (x * dt[..., None]).reshape(b, c, CHUNK, G, K, P)
    a = (dt * A).reshape(b, c, CHUNK, G, K)
    Bc = Bm.reshape(b, c, CHUNK, G, N)
    Cc = Cm.reshape(b, c, CHUNK, G, N)
    a_cs = jnp.cumsum(a, axis=2)
    seg = a_cs[:, :, :, None] - a_cs[:, :, None, :]
    causal = jnp.tril(jnp.ones((CHUNK, CHUNK), dtype=bool))[None, None, :, :, None, None]
    decay = jnp.exp(jnp.where(causal, seg, -jnp.inf))
    scores = jnp.einsum('bclgn,bcsgn->bclsg', Cc, Bc)
    y_diag = jnp.einsum('bclsg,bclsgk,bcsgkp->bclgkp', scores, decay, xdt)
    decay_to_end = jnp.exp(a_cs[:, :, -1:] - a_cs)
    states = jnp.einsum('bclgn,bclgk,bclgkp->bcgkpn', Bc, decay_to_end, xdt)
    chunk_decay = jnp.exp(a_cs[:, :, -1])

    def step(h, inp):
        st, dc = inp
        return h * dc[..., None, None] + st, h

    h0 = jnp.zeros((b, G, K, P, N), x.dtype)
    _, prev = lax.scan(step, h0, (jnp.moveaxis(states, 1, 0), jnp.moveaxis(chunk_decay, 1, 0)))
    prev = jnp.moveaxis(prev, 0, 1)
    y_off = jnp.einsum('bclgn,bcgkpn,bclgk->bclgkp', Cc, prev, jnp.exp(a_cs))
    return (y_diag + y_off).reshape(b, s, G * K, P)


def hybrid_mixer(xn, w_in, b_gate, lru_conv_w, lru_conv_b, lru_w_a, lru_b_a, lru_w_x, lru_b_x,
                 lru_lambda, ssd_conv_w, ssd_conv_b, ssd_dt_bias, ssd_A_log, ssd_D, ssd_norm_g,
                 w_branch, w_out):
    bsz, s, _ = xn.shape
    f32 = jnp.float32
    proj = xn @ w_in
    lru_x, lru_gate, z, xbc, dt_raw, gates = _split(proj, IN_WIDTHS)
    u = causal_dw_conv(lru_x, lru_conv_w, lru_conv_b)
    h = rg_lru(u, lru_w_a, lru_b_a, lru_w_x, lru_b_x, lru_lambda)
    y_a = jax.nn.gelu(lru_gate) * h
    xbc = jax.nn.silu(causal_dw_conv(xbc, ssd_conv_w, ssd_conv_b))
    xs, Bm, Cm = _split(xbc, (SSD_INNER, SSD_GROUPS * SSD_STATE, SSD_GROUPS * SSD_STATE))
    dt = jax.nn.softplus(dt_raw.astype(f32) + ssd_dt_bias.astype(f32))
    A = -jnp.exp(ssd_A_log.astype(f32))
    xh = xs.astype(f32).reshape(bsz, s, SSD_HEADS, SSD_HEAD_DIM)
    y = ssd_scan(xh, dt, A,
                 Bm.astype(f32).reshape(bsz, s, SSD_GROUPS, SSD_STATE),
                 Cm.astype(f32).reshape(bsz, s, SSD_GROUPS, SSD_STATE))
    y = y + ssd_D.astype(f32)[:, None] * xh
    y = y.reshape(bsz, s, SSD_INNER) * jax.nn.silu(z.astype(f32))
    yg = y.reshape(bsz, s, SSD_GROUPS, SSD_INNER // SSD_GROUPS)
    yg = yg * lax.rsqrt(jnp.mean(yg * yg, axis=-1, keepdims=True) + EPS)
    y_b = (yg.reshape(bsz, s, SSD_INNER) * ssd_norm_g.astype(f32)).astype(xn.dtype)
    g = jax.nn.sigmoid(gates + b_gate)
    g_a, g_b = _split(g, (D_MODEL, D_MODEL))
    merged = g_a * (y_a @ w_branch[:LRU_WIDTH]) + g_b * (y_b @ w_branch[LRU_WIDTH:])
    return merged @ w_out


def swiglu(xn, w_ffn_in, w_ffn_out):
    gate, up = _split(xn @ w_ffn_in, (D_FF, D_FF))
    return (jax.nn.silu(gate) * up) @ w_ffn_out


def setup_inputs(seed: int = 0) -> dict:
    key = jax.random.key(seed)
    ks = jax.random.split(key, 24)
    nrm = lambda k, shape, scale: jax.random.normal(k, shape, jnp.float32) * scale
    L = DEPTH
    a_c = jax.random.uniform(ks[9], (L, LRU_WIDTH), jnp.float32, 0.9, 0.999)
    sig = a_c ** (1.0 / LRU_C)
    lru_lambda = jnp.log(sig) - jnp.log1p(-sig)
    dt0 = jnp.exp(jax.random.uniform(ks[12], (L, SSD_HEADS), jnp.float32, np.log(1e-3), np.log(1e-1)))
    ssd_dt_bias = dt0 + jnp.log(-jnp.expm1(-dt0))
    ssd_A_log = jnp.log(jax.random.uniform(ks[13], (L, SSD_HEADS), jnp.float32, 1.0, 16.0))
    w_branch = jnp.concatenate([
        nrm(ks[16], (L, LRU_WIDTH, D_MODEL), LRU_WIDTH ** -0.5),
        nrm(ks[17], (L, SSD_INNER, D_MODEL), SSD_INNER ** -0.5)], axis=1)
    return {
        "x": nrm(ks[0], (BATCH, SEQ, D_MODEL), 1.0),
        "norm1_g": 1.0 + nrm(ks[1], (L, D_MODEL), 0.02),
        "w_in": nrm(ks[2], (L, D_MODEL, IN_DIM), D_MODEL ** -0.5),
        "b_gate": nrm(ks[3], (L, N_BRANCH * D_MODEL), 0.02),
        "lru_conv_w": nrm(ks[4], (L, CONV_W, LRU_WIDTH), CONV_W ** -0.5),
        "lru_conv_b": nrm(ks[5], (L, LRU_WIDTH), 0.02),
        "lru_w_a": nrm(ks[6], (L, LRU_BLOCKS, LRU_BLOCK_DIM, LRU_BLOCK_DIM), LRU_BLOCK_DIM ** -0.5),
        "lru_b_a": nrm(ks[7], (L, LRU_WIDTH), 0.02),
        "lru_w_x": nrm(ks[8], (L, LRU_BLOCKS, LRU_BLOCK_DIM, LRU_BLOCK_DIM), LRU_BLOCK_DIM ** -0.5),
        "lru_b_x": nrm(ks[18], (L, LRU_WIDTH), 0.02),
        "lru_lambda": lru_lambda,
        "ssd_conv_w": nrm(ks[10], (L, CONV_W, SSD_CONV_DIM), CONV_W ** -0.5),
        "ssd_conv_b": nrm(ks[11], (L, SSD_CONV_DIM), 0.02),
        "ssd_dt_bias": ssd_dt_bias,
        "ssd_A_log": ssd_A_log,
        "ssd_D": 1.0 + nrm(ks[14], (L, SSD_HEADS), 0.02),
        "ssd_norm_g": 1.0 + nrm(ks[15], (L, SSD_INNER), 0.02),
        "w_branch": w_branch,
        "w_out": nrm(ks[19], (L, D_MODEL, D_MODEL), D_MODEL ** -0.5),
        "norm2_g": 1.0 + nrm(ks[20], (L, D_MODEL), 0.02),
        "w_ffn_in": nrm(ks[21], (L, D_MODEL, 2 * D_FF), D_MODEL ** -0.5),
        "w_ffn_out": nrm(ks[22], (L, D_FF, D_MODEL), D_FF ** -0.5),
        "norm_f": 1.0 + nrm(ks[23], (D_MODEL,), 0.02),
    }


def reference(x, norm1_g, w_in, b_gate, lru_conv_w, lru_conv_b, lru_w_a, lru_b_a, lru_w_x, lru_b_x,
              lru_lambda, ssd_conv_w, ssd_conv_b, ssd_dt_bias, ssd_A_log, ssd_D, ssd_norm_g,
              w_branch, w_out, norm2_g, w_ffn_in, w_ffn_out, norm_f):
    h = x
    for l in range(DEPTH):
        h = h + hybrid_mixer(rms_norm(h, norm1_g[l]), w_in[l], b_gate[l], lru_conv_w[l], lru_conv_b[l],
                             lru_w_a[l], lru_b_a[l], lru_w_x[l], lru_b_x[l], lru_lambda[l],
                             ssd_conv_w[l], ssd_conv_b[l], ssd_dt_bias[l], ssd_A_log[l], ssd_D[l],
                             ssd_norm_g[l], w_branch[l], w_out[l])
        h = h + swiglu(rms_norm(h, norm2_g[l]), w_ffn_in[l], w_ffn_out[l])
    return rms_norm(h, norm_f)
```

```python
import contextlib
import numpy as np
import concourse.bass as bass
import concourse.mybir as mybir
from concourse.bass_utils import run_bass_kernel_spmd

F32 = mybir.dt.float32
BF16 = mybir.dt.bfloat16
AF = mybir.ActivationFunctionType
ALU = mybir.AluOpType

NCORES = 8
D = 1024
KC = 8
SEQ = 8192
TSEG = 2048
TT = 512
NTILE = TSEG // TT
Q = 128
NQ = TT // Q
IN_DIM = 9248
DFF = 2816
FC = DFF // 128
OFF_LX, OFF_LG, OFF_Z, OFF_XBC, OFF_DT, OFF_G = 0, 1024, 2048, 4096, 7168, 7200
EPS = 1e-6
NPAR = 336
P_N1G, P_N2G, P_BG, P_LCW, P_LCB, P_LBA, P_LBX, P_LAM, P_SCW, P_SCB, P_SNG, P_NF, P_DTB, P_ALOG, P_D = (
    0, 8, 16, 32, 64, 72, 80, 88, 96, 192, 216, 232, 240, 272, 304)
WST = 2048 + 48

SAME_ENGINE_SYNC = True
WSPLIT = 8
N_DMA_SEMS = 8


class Buf:
    __slots__ = ("name", "w", "r", "rd")

    def __init__(self, name):
        self.name = name
        self.w = None
        self.r = {}
        self.rd = []


class Op:
    __slots__ = ("eng", "fn", "deps", "is_dma", "semi", "semv", "needed", "inc")

    def __init__(self, eng, fn, is_dma):
        self.eng = eng
        self.fn = fn
        self.deps = []
        self.is_dma = is_dma
        self.semi = None
        self.semv = None
        self.needed = False
        self.inc = 16


class Sched:
    ENGS = ("pe", "act", "dve", "pool", "sp")

    def __init__(self, nc):
        self.nc = nc
        self.ops = {e: [] for e in self.ENGS}
        self.dma_hist = {e: [] for e in self.ENGS}
        self.cc_count = 0

    def op(self, eng, fn, reads=(), writes=(), dma=False, cc=False):
        o = Op(eng, fn, dma or cc)
        deps = []
        for b in reads:
            if b.w is not None:
                deps.append((b.w, True))
        for b in writes:
            if b.w is not None:
                deps.append((b.w, False))
            deps.extend((x, False) for x in b.r.values())
            deps.extend((x, False) for x in b.rd)
        if cc:
            self.cc_count += 1
            o.semi = "cc"
            o.semv = self.cc_count
            o.inc = 1
        elif dma:
            h = self.dma_hist[eng]
            k = len(h)
            o.semi = k % N_DMA_SEMS
            o.semv = 16 * (k // N_DMA_SEMS + 1)
            if k >= N_DMA_SEMS:
                deps.append((h[k - N_DMA_SEMS], True))
            h.append(o)
        seen = set()
        for d, raw in deps:
            if d is o or id(d) in seen:
                continue
            if (not d.is_dma) and (not o.is_dma) and d.eng == eng:
                if eng == "pe" or not SAME_ENGINE_SYNC:
                    continue
            seen.add(id(d))
            o.deps.append(d)
            d.needed = True
        for b in reads:
            if o.is_dma:
                b.rd.append(o)
            else:
                b.r[eng] = o
        for b in writes:
            b.w = o
            b.r = {}
            b.rd = []
        self.ops[eng].append(o)
        return o

    def emit(self):
        nc = self.nc
        for e in self.ENGS:
            c = 0
            for o in self.ops[e]:
                if o.is_dma:
                    continue
                if o.needed:
                    c += 1
                    o.semv = c
        with contextlib.ExitStack() as st:
            csem = {e: st.enter_context(nc.semaphore("c_" + e)) for e in ("pe", "act", "dve", "pool")}
            dsem = {e: [st.enter_context(nc.semaphore("d_%s%d" % (e, i))) for i in range(N_DMA_SEMS)]
                    for e in ("sp", "pool")}
            ccsem = st.enter_context(nc.semaphore("ccs"))
            block = st.enter_context(nc.Block())

            def semof(d):
                if d.is_dma:
                    return ccsem if d.semi == "cc" else dsem[d.eng][d.semi]
                return csem[d.eng]

            def run(e, handle):
                waited = {}
                last = {}
                for o in self.ops[e]:
                    for d in o.deps:
                        s = semof(d)
                        if waited.get(s.num, 0) >= d.semv:
                            continue
                        waited[s.num] = d.semv
                        handle.wait_ge(s, d.semv)
                    ins = o.fn(handle)
                    if o.is_dma:
                        s = semof(o)
                        ins.then_inc(s, o.inc)
                        last[s.num] = (s, o.semv)
                    elif o.needed:
                        ins.then_inc(csem[e], 1)
                for s, v in last.values():
                    if waited.get(s.num, 0) < v:
                        handle.wait_ge(s, v)

            @block.tensor
            def _(h):
                run("pe", h)

            @block.scalar
            def _(h):
                run("act", h)

            @block.vector
            def _(h):
                run("dve", h)

            @block.gpsimd
            def _(h):
                run("pool", h)

            @block.sync
            def _(h):
                run("sp", h)


def build_program(nlayers=2, exchange=True, dbg=None):
    nc = bass.Bass("TRN2", target_bir_lowering=False)
    S = Sched(nc)
    L = nlayers

    def din(name, shape):
        return nc.dram_tensor(name, list(shape), F32, kind="ExternalInput").ap()

    xT_d = din("xT", [D, TSEG])
    xh_d = din("xh", [128, 24])
    pr_d = din("pr", [2, 128, NPAR])
    msk_d = din("msk", [128, 16])
    cst_d = din("cst", [128, 384])
    w_in_d = din("w_in", [2, D, IN_DIM])
    wbda_d = din("wbd_a", [2, KC, 128, 128])
    wbdx_d = din("wbd_x", [2, KC, 128, 128])
    w_br_d = din("w_branch", [2, 3072, D])
    w_out_d = din("w_out", [2, D, D])
    w_fi_d = din("w_ffn_in", [2, D, 2 * DFF])
    w_fo_d = din("w_ffn_out", [2, DFF, D])
    yT_d = nc.dram_tensor("yT", [D, TSEG], F32, kind="ExternalOutput").ap()
    xs_d = nc.dram_tensor("xscr", [D, TSEG], F32).ap()
    bst_d = [nc.dram_tensor("bst%d" % l, [128, WST], F32).ap() for l in range(2)]
    gst_d = [nc.dram_tensor("gst%d" % l, [NCORES * 128, WST], F32).ap() for l in range(2)]
    bh_d = nc.dram_tensor("bh", [128, 24], F32).ap()
    gh_d = nc.dram_tensor("gh", [NCORES * 128, 24], F32).ap()
    b_xs = [[Buf("xscr%d_%d" % (t, k)) for k in range(KC)] for t in range(NTILE)]
    b_bst = [Buf("bst0"), Buf("bst1")]
    b_bsm = [Buf("bsm0"), Buf("bsm1")]
    b_gst = [Buf("gst0"), Buf("gst1")]
    b_bh, b_gh = Buf("bh"), Buf("gh")

    with contextlib.ExitStack() as st:
        def sb(name, shape, dt=F32):
            return st.enter_context(nc.sbuf_tensor("s_" + name, list(shape), dt))

        Xt = sb("Xt", [128, KC, TT]); bX = [Buf("X%d" % k) for k in range(KC)]
        xn = sb("xn", [128, KC, TT], BF16); bxn = [Buf("xn%d" % k) for k in range(KC)]
        Xh = sb("Xh", [128, KC, 3]); bXh = Buf("Xh")
        xnh = sb("xnh", [128, KC, 3], BF16); bxnh = Buf("xnh")
        NW = 3
        wb = [sb("wb%d" % i, [128, 4096], BF16) for i in range(NW)]
        bwb = [Buf("wb%d" % i) for i in range(NW)]
        cin = [sb("cin%d" % i, [128, TT + 4]) for i in range(2)]; bcin = [Buf("cin0"), Buf("cin1")]
        acc = [sb("acc%d" % i, [128, TT]) for i in range(2)]; bacc = [Buf("acc0"), Buf("acc1")]
        Hh = sb("Hh", [128, 32, 4]); bH = [Buf("H%d" % c) for c in range(32)]
        lt = {n: sb("lt_" + n, [128, TT]) for n in ("u", "r", "i", "a", "h", "gg")}
        blt = {n: Buf("lt_" + n) for n in lt}
        ubf = sb("ubf", [128, TT], BF16); bubf = Buf("ubf")
        hst = sb("hst", [128, KC]); bhst = Buf("hst")
        rsum = sb("rsum", [128, KC]); brsum = Buf("rsum")
        rs_t = sb("rs_t", [128, KC]); brs_t = Buf("rs_t")
        y_a = sb("y_a", [128, KC, TT], BF16); bya = [Buf("ya%d" % c) for c in range(KC)]
        reg = sb("reg", [128, 24, TT], BF16); breg = [Buf("reg%d" % c) for c in range(24)]
        y_b = sb("y_b", [128, 16, TT], BF16); byb = [Buf("yb%d" % c) for c in range(16)]
        mg = sb("mg", [128, KC, TT], BF16); bmg = [Buf("mg%d" % c) for c in range(KC)]
        sm_names = ("dtr", "e1", "dt", "av", "eacs", "dte", "cd", "w2", "dif")
        tq = {n: sb("tq_" + n, [128, NQ, 32]) for n in sm_names}
        btq = {n: [Buf("tq_%s%d" % (n, q)) for q in range(NQ)] for n in sm_names}
        totacc = sb("totacc", [128, 32]); btot = Buf("totacc")
        def dbl(name, shape, dt):
            return [sb("%s%d" % (name, i), shape, dt) for i in range(2)], [Buf("%s%d" % (name, i)) for i in range(2)]
        xsT, bxsT = dbl("xsT", [128, 512], BF16)
        xdt, bxdt = dbl("xdt", [128, 512], BF16)
        xdd, bxdd = dbl("xdd", [128, 512], BF16)
        xsD, bxsD = dbl("xsD", [128, 512], BF16)
        BT, bBT = dbl("BT", [128, 128], BF16)
        Rt, bRt = dbl("Rt", [128, 512], F32)
        Et, bEt = dbl("Et", [128, 512], BF16)
        Lm, bLm = dbl("Lm", [128, 512], BF16)
        smk = sb("smk", [128, 128]); bsmk = Buf("smk")
        yo = sb("yo", [128, 512]); byo = Buf("yo")
        yy = sb("yy", [128, 512]); byy = Buf("yy")
        sz = sb("sz", [128, 512]); bsz = Buf("sz")
        ynb = sb("ynb", [128, 512], BF16); bynb = Buf("ynb")
        sq4 = sb("sq4", [128, 4]); bsq4 = Buf("sq4")
        Sst = sb("Sst", [128, 2048]); bS = [Buf("S%d" % g) for g in range(4)]
        Sbf = sb("Sbf", [128, 2048], BF16); bSbf = [Buf("Sbf%d" % g) for g in range(4)]
        Stmp = sb("Stmp", [128, 512]); bStmp = Buf("Stmp")
        ga = sb("ga", [128, TT]); bga = Buf("ga")
        gb = sb("gb", [128, TT]); bgb = Buf("gb")
        t1 = sb("t1", [128, TT]); bt1 = Buf("t1")
        t2 = sb("t2", [128, TT]); bt2 = Buf("t2")
        nsq = [sb("nsq%d" % i, [128, TT], BF16) for i in range(2)]; bnsq = [Buf("nsq0"), Buf("nsq1")]
        rstd = sb("rstd", [128, TT]); brstd = Buf("rstd")
        cst = sb("cst", [128, 384]); bcst = Buf("cst")
        U_f = cst[:, 0:128]
        Ls_f = cst[:, 128:256]
        id_f = cst[:, 256:384]
        ident = sb("ident", [128, 128], BF16); bident = Buf("ident")
        ones_b = sb("ones_b", [128, 128], BF16); bones = Buf("ones")
        ones_f = sb("ones_f", [128, 128]); bonesf = Buf("onesf")
        msk = sb("msk", [128, 16]); bmsk = Buf("msk")
        pr = sb("pr", [128, NPAR]); bpr = Buf("pr")
        kk = sb("kk", [128, KC]); bkk = Buf("kk")
        Ab = sb("Ab", [128, 32]); bAb = Buf("Ab")
        ptmp = sb("ptmp", [128, 32]); bptmp = Buf("ptmp")
        bda = sb("bda", [128, KC, 128], BF16); bbda = Buf("bda")
        bdx = sb("bdx", [128, KC, 128], BF16); bbdx = Buf("bdx")
        wdt = sb("wdt", [128, KC, 32], BF16); bwdt = Buf("wdt")
        smalls = sb("smalls", [128, 48]); bsmalls = Buf("smalls")
        gsm = sb("gsm", [128, NCORES, 48]); bgsm = Buf("gsm")
        coef = sb("coef", [128, 32]); bcoef = Buf("coef")
        hx = sb("hx", [128, NCORES, 24]); bhx = Buf("hx")

        pbank = [st.enter_context(nc.psum_tensor("pb%d" % i, [128, 512], F32)) for i in range(8)]
        bpb = [Buf("pb%d" % i) for i in range(8)]
        pcount = [0]

        def pnext():
            i = pcount[0] % 7
            pcount[0] += 1
            return pbank[i], bpb[i]
        ph, bph = pbank[7], bpb[7]

        wcount = [0]

        def wnext():
            i = wcount[0] % NW
            wcount[0] += 1
            return wb[i], bwb[i]

        def MM(out, lhsT, rhs, start, stop, r, w):
            S.op("pe", lambda e: e.matmul(out, lhsT=lhsT, rhs=rhs, start=start, stop=stop), reads=r, writes=w)

        def TR(out, in_, r, w):
            S.op("pe", lambda e: e.transpose(out=out, in_=in_, identity=ident[:]), reads=r + [bident], writes=w)

        def ACT(out, in_, func, r, w, bias=None, scale=None, accum=None):
            kw = {}
            if bias is not None:
                kw["bias"] = bias
            if scale is not None:
                kw["scale"] = scale
            if accum is not None:
                kw["accum_out"] = accum
            S.op("act", lambda e: e.activation(out=out, in_=in_, func=func, **kw), reads=r, writes=w)

        def TTo(eng, out, in0, in1, op, r, w):
            S.op(eng, lambda e: e.tensor_tensor(out=out, in0=in0, in1=in1, op=op), reads=r, writes=w)

        def TS(eng, out, in0, s1, s2, op0, op1, r, w):
            if s2 is None:
                S.op(eng, lambda e: e.tensor_scalar(out=out, in0=in0, scalar1=s1, scalar2=None, op0=op0), reads=r, writes=w)
            else:
                S.op(eng, lambda e: e.tensor_scalar(out=out, in0=in0, scalar1=s1, scalar2=s2, op0=op0, op1=op1), reads=r, writes=w)

        def STT(out, in0, scalar, in1, op0, op1, r, w):
            S.op("dve", lambda e: e.scalar_tensor_tensor(out=out, in0=in0, scalar=scalar, in1=in1, op0=op0, op1=op1),
                 reads=r, writes=w)

        def CP(eng, out, in_, r, w):
            S.op(eng, lambda e: e.tensor_copy(out=out, in_=in_), reads=r, writes=w)

        def MSET(eng, ap, val, w):
            S.op(eng, lambda e: e.memset(ap, val), writes=w)

        def DMA(q, out, in_, r, w):
            if q == "pool" and len(out.shape) == 3 and WSPLIT:
                nk = out.shape[1]
                for k0 in range(0, nk, WSPLIT):
                    k1 = min(nk, k0 + WSPLIT)
                    S.op(q, lambda e, k0=k0, k1=k1: e.dma_start(out=out[:, k0:k1, :], in_=in_[:, k0:k1, :]),
                         reads=r, writes=w, dma=True)
                return
            S.op(q, lambda e: e.dma_start(out=out, in_=in_), reads=r, writes=w, dma=True)

        def h8(ap):
            return ap.rearrange("p (h d) -> p h d", h=8)

        def bc8(ap8):
            return ap8.unsqueeze(2).to_broadcast([128, 8, 64])

        DMA("sp", cst[:], cst_d[:, :], [], [bcst])
        DMA("sp", msk[:], msk_d[:, :], [], [bmsk])
        CP("dve", ident[:], id_f, [bcst], [bident])
        MSET("dve", ones_b[:], 1.0, [bones])
        MSET("dve", ones_f[:], 1.0, [bonesf])
        DMA("sp", Xh[:], xh_d.rearrange("p (k i) -> p k i", k=KC), [], [bXh])

        def kchunks(wd, l, c0, ncols):
            return wd[l].rearrange("(k p) n -> p k n", p=128)[:, :, c0:c0 + ncols]

        def rmsnorm(src, bsrc, width, gcol, dst, bdst):
            pt, bpt = pnext()
            for k in range(KC):
                i = k % 2
                ACT(nsq[i][:, 0:width], src[:, k, 0:width], AF.Square, [bsrc[k] if isinstance(bsrc, list) else bsrc], [bnsq[i]])
                MM(pt[:, 0:width], ones_b[:], nsq[i][:, 0:width], k == 0, k == KC - 1, [bones, bnsq[i]], [bpt])
            ACT(rstd[:, 0:width], pt[:, 0:width], AF.Sqrt, [bpt], [brstd], scale=1.0 / D, bias=EPS)
            S.op("dve", lambda e: e.reciprocal(out=rstd[:, 0:width], in_=rstd[:, 0:width]), reads=[brstd], writes=[brstd])
            for k in range(KC):
                STT(dst[:, k, 0:width], src[:, k, 0:width], pr[:, gcol + k:gcol + k + 1], rstd[:, 0:width],
                    ALU.mult, ALU.mult,
                    [bsrc[k] if isinstance(bsrc, list) else bsrc, bpr, brstd],
                    [bdst[k] if isinstance(bdst, list) else bdst])

        def layer_setup(l):
            DMA("sp", pr[:], pr_d[l], [], [bpr])
            ACT(kk[:], pr[:, P_LAM:P_LAM + 8], AF.Exp, [bpr], [bkk], scale=-1.0)
            ACT(kk[:], kk[:], AF.Ln, [bkk], [bkk], bias=1.0)
            TS("dve", kk[:], kk[:], -8.0, None, ALU.mult, None, [bkk], [bkk])
            ACT(Ab[:], pr[:, P_ALOG:P_ALOG + 32], AF.Exp, [bpr], [bAb])
            TS("dve", Ab[:], Ab[:], -1.0, None, ALU.mult, None, [bAb], [bAb])
            DMA("pool", bda[:], wbda_d[l].rearrange("c i j -> i c j"), [], [bbda])
            DMA("pool", bdx[:], wbdx_d[l].rearrange("c i j -> i c j"), [], [bbdx])
            DMA("pool", wdt[:], kchunks(w_in_d, l, OFF_DT, 32), [], [bwdt])

        def conv_chunk(l, t, ci, wt, bwt, wcol, post):
            wv = wt[:, 0:4096].rearrange("p (k n) -> p k n", k=KC)
            pt, bpt = pnext()
            for k in range(KC):
                MM(pt[:], wv[:, k, wcol:wcol + 128], xn[:, k, :], k == 0, k == KC - 1, [bwt, bxn[k]], [bpt])
            i = ci % 2
            if t == 0:
                for k in range(KC):
                    MM(ph[:, ci * 4:ci * 4 + 3], wv[:, k, wcol:wcol + 128], xnh[:, k, 0:3], k == 0, k == KC - 1,
                       [bwt, bxnh], [bph])
                ACT(cin[i][:, 0:3], ph[:, ci * 4:ci * 4 + 3], AF.Copy, [bph], [bcin[i]])
            else:
                CP("dve", cin[i][:, 0:3], Hh[:, ci, 0:3], [bH[ci]], [bcin[i]])
            ACT(cin[i][:, 3:3 + TT], pt[:], AF.Copy, [bpt], [bcin[i]])
            CP("dve", Hh[:, ci, 0:3], cin[i][:, TT:TT + 3], [bcin[i]], [bH[ci]])
            if ci < 8:
                wc = lambda kk_: pr[:, P_LCW + kk_ * 8 + ci:P_LCW + kk_ * 8 + ci + 1]
            else:
                wc = lambda kk_: pr[:, P_SCW + kk_ * 24 + (ci - 8):P_SCW + kk_ * 24 + (ci - 8) + 1]
            TS("dve", acc[i][:], cin[i][:, 0:TT], wc(0), None, ALU.mult, None, [bcin[i], bpr], [bacc[i]])
            for kk_ in range(1, 4):
                STT(acc[i][:], cin[i][:, kk_:kk_ + TT], wc(kk_), acc[i][:], ALU.mult, ALU.add,
                    [bcin[i], bpr, bacc[i]], [bacc[i]])
            post(acc[i], bacc[i])

        def run_tile(l, t, full, last):
            src = xT_d if l == 0 else xs_d
            srcv = src.rearrange("(k p) t -> p k t", p=128)
            for k in range(KC):
                DMA("sp", Xt[:, k, :], srcv[:, k, t * TT:(t + 1) * TT], [b_xs[t][k]], [bX[k]])
            rmsnorm(Xt, bX, TT, P_N1G, xn, bxn)
            if t == 0:
                rmsnorm(Xh, bXh, 3, P_N1G, xnh, bxnh)

            stop = (dbg or {}).get("stop", 99)
            for grp in range(2 if stop >= 1 else 0):
                wt, bwt = wnext()
                DMA("pool", wt[:, 0:4096].rearrange("p (k n) -> p k n", k=KC), kchunks(w_in_d, l, OFF_LX + grp * 512, 512), [], [bwt])
                if full:
                    wg, bwg = wnext()
                    DMA("pool", wg[:, 0:4096].rearrange("p (k n) -> p k n", k=KC), kchunks(w_in_d, l, OFF_LG + grp * 512, 512), [], [bwg])
                    wgv = wg[:, 0:4096].rearrange("p (k n) -> p k n", k=KC)
                for cc in range(4):
                    c = grp * 4 + cc

                    def post(a_, ba_, c=c):
                        ACT(lt["u"][:], a_[:], AF.Identity, [ba_, bpr], [blt["u"]], bias=pr[:, P_LCB + c:P_LCB + c + 1])
                        CP("dve", ubf[:], lt["u"][:], [blt["u"]], [bubf])
                    conv_chunk(l, t, c, wt, bwt, cc * 128, post)
                    pr_, bpr_ = pnext()
                    MM(pr_[:], bda[:, c, :], ubf[:], True, True, [bbda, bubf], [bpr_])
                    pi_, bpi_ = pnext()
                    MM(pi_[:], bdx[:, c, :], ubf[:], True, True, [bbdx, bubf], [bpi_])
                    if full:
                        ACT(lt["r"][:], pr_[:], AF.Sigmoid, [bpr_, bpr], [blt["r"]], bias=pr[:, P_LBA + c:P_LBA + c + 1])
                    else:
                        ACT(lt["r"][:], pr_[:], AF.Sigmoid, [bpr_, bpr], [blt["r"], brs_t], bias=pr[:, P_LBA + c:P_LBA + c + 1],
                            accum=rs_t[:, c:c + 1])
                    ACT(lt["i"][:], pi_[:], AF.Sigmoid, [bpi_, bpr], [blt["i"]], bias=pr[:, P_LBX + c:P_LBX + c + 1])
                    ACT(lt["a"][:], lt["r"][:], AF.Exp, [blt["r"], bkk], [blt["a"]], scale=kk[:, c:c + 1])
                    TTo("dve", lt["r"][:], lt["a"][:], lt["a"][:], ALU.mult, [blt["a"]], [blt["r"]])
                    ACT(lt["r"][:], lt["r"][:], AF.Sqrt, [blt["r"]], [blt["r"]], scale=-1.0, bias=1.0)
                    TTo("dve", lt["i"][:], lt["i"][:], lt["u"][:], ALU.mult, [blt["i"], blt["u"]], [blt["i"]])
                    TTo("dve", lt["i"][:], lt["i"][:], lt["r"][:], ALU.mult, [blt["i"], blt["r"]], [blt["i"]])
                    S.op("dve", lambda e, c=c: e.tensor_tensor_scan(out=lt["h"][:], data0=lt["a"][:], data1=lt["i"][:],
                                                                   initial=hst[:, c:c + 1], op0=ALU.mult, op1=ALU.add),
                         reads=[blt["a"], blt["i"], bhst], writes=[blt["h"]])
                    CP("dve", hst[:, c:c + 1], lt["h"][:, TT - 1:TT], [blt["h"]], [bhst])
                    if full:
                        pg_, bpg_ = pnext()
                        for k in range(KC):
                            MM(pg_[:], wgv[:, k, cc * 128:cc * 128 + 128], xn[:, k, :], k == 0, k == KC - 1, [bwg, bxn[k]], [bpg_])
                        ACT(lt["gg"][:], pg_[:], AF.Gelu_apprx_tanh, [bpg_], [blt["gg"]])
                        TTo("dve", y_a[:, c, :], lt["gg"][:], lt["h"][:], ALU.mult, [blt["gg"], blt["h"]], [bya[c]])
            if not full:
                TTo("dve", rsum[:], rsum[:], rs_t[:], ALU.add, [brsum, brs_t], [brsum])

            nxbc = 24 if full else 20
            for grp in range(nxbc // 4 if stop >= 2 else 0):
                wt, bwt = wnext()
                DMA("pool", wt[:, 0:4096].rearrange("p (k n) -> p k n", k=KC), kchunks(w_in_d, l, OFF_XBC + grp * 512, 512), [], [bwt])
                for cc in range(4):
                    c = grp * 4 + cc

                    def post(a_, ba_, c=c):
                        ACT(reg[:, c, :], a_[:], AF.Silu, [ba_, bpr], [breg[c]], bias=pr[:, P_SCB + c:P_SCB + c + 1])
                    conv_chunk(l, t, 8 + c, wt, bwt, cc * 128, post)

            for q in range(NQ if stop >= 3 else 0):
                qs = slice(q * Q, (q + 1) * Q)
                pt, bpt = pnext()
                for k in range(KC):
                    MM(pt[:, 0:32], xn[:, k, qs], wdt[:, k, :], k == 0, k == KC - 1, [bxn[k], bwdt], [bpt])
                TTo("dve", tq["dtr"][:, q, :], pt[:, 0:32], pr[:, P_DTB:P_DTB + 32], ALU.add, [bpt, bpr], [btq["dtr"][q]])
                ACT(tq["e1"][:, q, :], tq["dtr"][:, q, :], AF.Exp, [btq["dtr"][q]], [btq["e1"][q]])
                ACT(tq["dt"][:, q, :], tq["e1"][:, q, :], AF.Ln, [btq["e1"][q]], [btq["dt"][q]], bias=1.0)
                TTo("dve", tq["av"][:, q, :], tq["dt"][:, q, :], Ab[:], ALU.mult, [btq["dt"][q], bAb], [btq["av"][q]])
                p2, bp2 = pnext()
                MM(p2[:, 0:32], U_f, tq["av"][:, q, :], True, False, [bcst, btq["av"][q]], [bp2])
                MM(p2[:, 32:64], ones_f[:], tq["av"][:, q, :], False, True, [bonesf, btq["av"][q]], [bp2])
                ACT(tq["eacs"][:, q, :], p2[:, 0:32], AF.Exp, [bp2], [btq["eacs"][q]])
                ACT(tq["cd"][:, q, :], p2[:, 32:64], AF.Exp, [bp2], [btq["cd"][q]])
                CP("dve", tq["dif"][:, q, :], p2[:, 32:64], [bp2], [btq["dif"][q]])
                if not full:
                    TTo("dve", totacc[:], totacc[:], tq["dif"][:, q, :], ALU.add, [btot, btq["dif"][q]], [btot])
                TTo("dve", tq["dif"][:, q, :], tq["dif"][:, q, :], p2[:, 0:32], ALU.subtract, [btq["dif"][q], bp2], [btq["dif"][q]])
                ACT(tq["dte"][:, q, :], tq["dif"][:, q, :], AF.Exp, [btq["dif"][q]], [btq["dte"][q]])
                TTo("dve", tq["w2"][:, q, :], tq["dt"][:, q, :], tq["dte"][:, q, :], ALU.mult, [btq["dt"][q], btq["dte"][q]], [btq["w2"][q]])

            it = 0
            for g in range(4 if stop >= 4 else 0):
                gh = slice(g * 8, (g + 1) * 8)
                Sg = Sst[:, g * 512:(g + 1) * 512]
                Sbg = Sbf[:, g * 512:(g + 1) * 512]
                if full and (dbg or {}).get("sub", 9) >= 4:
                    wz, bwz = wnext()
                    DMA("pool", wz[:, 0:4096].rearrange("p (k n) -> p k n", k=KC), kchunks(w_in_d, l, OFF_Z + g * 512, 512), [], [bwz])
                    wzv = wz[:, 0:4096].rearrange("p (k n) -> p k n", k=KC)
                for q in range(NQ):
                    qs = slice(q * Q, (q + 1) * Q)
                    j = it % 2
                    it += 1
                    sub = (dbg or {}).get("sub", 9)
                    pt, bpt = pnext()
                    ptb = pt[:].bitcast(BF16)
                    if sub >= 0.1:
                        for i4 in range(4):
                            TR(ptb[:, i4 * 128:(i4 + 1) * 128], reg[:, g * 4 + i4, qs], [breg[g * 4 + i4]], [bpt])
                        ACT(xsT[j][:], ptb[:, 0:512], AF.Copy, [bpt], [bxsT[j]])
                    if sub >= 0.2:
                        TTo("dve", h8(xdt[j][:]), h8(xsT[j][:]), bc8(tq["dt"][:, q, gh]), ALU.mult, [bxsT[j], btq["dt"][q]], [bxdt[j]])
                    if sub >= 0.3:
                        TTo("dve", h8(xdd[j][:]), h8(xsT[j][:]), bc8(tq["w2"][:, q, gh]), ALU.mult, [bxsT[j], btq["w2"][q]], [bxdd[j]])
                    p3, bp3 = pnext()
                    p3b = p3[:].bitcast(BF16)
                    if sub >= 0.4:
                        TR(p3b[:, 0:128], reg[:, 16 + g, qs], [breg[16 + g]], [bp3])
                        ACT(BT[j][:], p3b[:, 0:128], AF.Copy, [bp3], [bBT[j]])
                    if full and sub >= 2:
                        TTo("dve", h8(xsD[j][:]), h8(xsT[j][:]), bc8(pr[:, P_D + g * 8:P_D + g * 8 + 8]), ALU.mult, [bxsT[j], bpr], [bxsD[j]])
                        p4, bp4 = pnext()
                        MM(p4[:, 0:128], reg[:, 16 + g, qs], reg[:, 20 + g, qs], True, True, [breg[16 + g], breg[20 + g]], [bp4])
                        TTo("dve", smk[:], p4[:, 0:128], U_f, ALU.mult, [bp4, bcst], [bsmk])
                        p5, bp5 = pnext()
                        for hh in range(8):
                            MM(p5[:, hh * 64:(hh + 1) * 64], reg[:, 20 + g, qs], Sbg[:, hh * 64:(hh + 1) * 64], hh == 0, hh == 7,
                               [breg[20 + g], bSbf[g]], [bp5])
                        TTo("dve", h8(yo[:]), h8(p5[:]), bc8(tq["eacs"][:, q, gh]), ALU.mult, [bp5, btq["eacs"][q]], [byo])
                    if full and sub >= 3:
                        p6, bp6 = pnext()
                        MM(p6[:], ident[:], xsD[j][:], True, False, [bident, bxsD[j]], [bp6])
                        for half in range(2):
                            jj = half
                            hs = slice(g * 8 + half * 4, g * 8 + half * 4 + 4)
                            TTo("dve", Rt[jj][:].rearrange("p (h l) -> p h l", h=4),
                                U_f.unsqueeze(1).to_broadcast([128, 4, 128]),
                                tq["av"][:, q, hs].unsqueeze(2).to_broadcast([128, 4, 128]), ALU.mult,
                                [bcst, btq["av"][q]], [bRt[jj]])
                            p7, bp7 = pnext()
                            MM(p7[:], Ls_f, Rt[jj][:], True, True, [bcst, bRt[jj]], [bp7])
                            ACT(Et[jj][:], p7[:], AF.Exp, [bp7], [bEt[jj]])
                            TTo("dve", Lm[jj][:].rearrange("p (h l) -> p h l", h=4), Et[jj][:].rearrange("p (h l) -> p h l", h=4),
                                smk[:].unsqueeze(1).to_broadcast([128, 4, 128]), ALU.mult, [bEt[jj], bsmk], [bLm[jj]])
                            for h4 in range(4):
                                hh = half * 4 + h4
                                MM(p6[:, hh * 64:(hh + 1) * 64], Lm[jj][:, h4 * 128:(h4 + 1) * 128], xdt[j][:, hh * 64:(hh + 1) * 64],
                                   False, (hh == 7), [bLm[jj], bxdt[j]], [bp6])
                        TTo("dve", yy[:], p6[:], yo[:], ALU.add, [bp6, byo], [byy])
                    if full and sub >= 4:
                        p8, bp8 = pnext()
                        for k in range(KC):
                            MM(p8[:], xn[:, k, qs], wzv[:, k, :], k == 0, k == KC - 1, [bxn[k], bwz], [bp8])
                        ACT(sz[:], p8[:], AF.Silu, [bp8], [bsz])
                        TTo("dve", yy[:], yy[:], sz[:], ALU.mult, [byy, bsz], [byy])
                        ACT(sz[:], yy[:], AF.Square, [byy], [bsz, bsq4], accum=sq4[:, 0:1])
                        ACT(sq4[:, 1:2], sq4[:, 0:1], AF.Sqrt, [bsq4], [bsq4], scale=1.0 / 512.0, bias=EPS)
                        S.op("dve", lambda e: e.reciprocal(out=sq4[:, 2:3], in_=sq4[:, 1:2]), reads=[bsq4], writes=[bsq4])
                        TS("dve", ynb[:], yy[:], sq4[:, 2:3], None, ALU.mult, None, [byy, bsq4], [bynb])
                        p9, bp9 = pnext()
                        p9b = p9[:].bitcast(BF16)
                        for i4 in range(4):
                            TR(p9b[:, i4 * 128:(i4 + 1) * 128], ynb[:, i4 * 128:(i4 + 1) * 128], [bynb], [bp9])
                        for i4 in range(4):
                            cch = g * 4 + i4
                            ACT(y_b[:, cch, qs], p9b[:, i4 * 128:(i4 + 1) * 128], AF.Copy, [bp9, bpr], [byb[cch]],
                                scale=pr[:, P_SNG + cch:P_SNG + cch + 1])
                    pA, bpA = pnext()
                    for hh in range(8 if sub >= 0.5 else 0):
                        MM(pA[:, hh * 64:(hh + 1) * 64], BT[j][:], xdd[j][:, hh * 64:(hh + 1) * 64], hh == 0, hh == 7,
                           [bBT[j], bxdd[j]], [bpA])
                    if sub >= 0.6:
                        TTo("dve", h8(Stmp[:]), h8(Sg), bc8(tq["cd"][:, q, gh]), ALU.mult, [bS[g], btq["cd"][q]], [bStmp])
                        TTo("dve", Sg, Stmp[:], pA[:], ALU.add, [bStmp, bpA], [bS[g]])
                    if full and sub >= 0.6:
                        ACT(Sbg, Sg, AF.Copy, [bS[g]], [bSbf[g]])
            if not full:
                return

            wbrv = w_br_d[l].rearrange("(k p) n -> p k n", p=128)
            for m in range(KC if stop >= 5 else 0):
                wgt, bwgt = wnext()
                wgv2 = wgt[:, 0:2048].rearrange("p (k n) -> p k n", k=KC)
                DMA("pool", wgv2[:, :, 0:128], kchunks(w_in_d, l, OFF_G + m * 128, 128), [], [bwgt])
                DMA("pool", wgv2[:, :, 128:256], kchunks(w_in_d, l, OFF_G + 1024 + m * 128, 128), [], [bwgt])
                wbt, bwbt = wnext()
                wbv = wbt[:, 0:3072].rearrange("p (k n) -> p k n", k=24)
                DMA("pool", wbv, wbrv[:, :, m * 128:(m + 1) * 128], [], [bwbt])
                pa_, bpa_ = pnext()
                for k in range(KC):
                    MM(pa_[:], wgv2[:, k, 0:128], xn[:, k, :], k == 0, k == KC - 1, [bwgt, bxn[k]], [bpa_])
                ACT(ga[:], pa_[:], AF.Sigmoid, [bpa_, bpr], [bga], bias=pr[:, P_BG + m:P_BG + m + 1])
                pb_, bpb_ = pnext()
                for k in range(KC):
                    MM(pb_[:], wgv2[:, k, 128:256], xn[:, k, :], k == 0, k == KC - 1, [bwgt, bxn[k]], [bpb_])
                ACT(gb[:], pb_[:], AF.Sigmoid, [bpb_, bpr], [bgb], bias=pr[:, P_BG + 8 + m:P_BG + 8 + m + 1])
                pc_, bpc_ = pnext()
                for k in range(KC):
                    MM(pc_[:], wbv[:, k, :], y_a[:, k, :], k == 0, k == KC - 1, [bwbt, bya[k]], [bpc_])
                TTo("dve", t1[:], pc_[:], ga[:], ALU.mult, [bpc_, bga], [bt1])
                pd_, bpd_ = pnext()
                for k in range(16):
                    MM(pd_[:], wbv[:, 8 + k, :], y_b[:, k, :], k == 0, k == 15, [bwbt, byb[k]], [bpd_])
                TTo("dve", t2[:], pd_[:], gb[:], ALU.mult, [bpd_, bgb], [bt2])
                TTo("dve", mg[:, m, :], t1[:], t2[:], ALU.add, [bt1, bt2], [bmg[m]])
            for grp in range(2 if stop >= 6 else 0):
                wt, bwt = wnext()
                wv = wt[:, 0:4096].rearrange("p (k n) -> p k n", k=KC)
                DMA("pool", wv, kchunks(w_out_d, l, grp * 512, 512), [], [bwt])
                for cc in range(4):
                    m = grp * 4 + cc
                    pt, bpt = pnext()
                    for k in range(KC):
                        MM(pt[:], wv[:, k, cc * 128:(cc + 1) * 128], mg[:, k, :], k == 0, k == KC - 1, [bwt, bmg[k]], [bpt])
                    TTo("dve", Xt[:, m, :], Xt[:, m, :], pt[:], ALU.add, [bX[m], bpt], [bX[m]])
            if stop >= 7:
                rmsnorm(Xt, bX, TT, P_N2G, xn, bxn)
            for pr2 in range(FC // 2 if stop >= 7 else 0):
                wt, bwt = wnext()
                wv = wt[:, 0:4096].rearrange("p (k n) -> p k n", k=KC)
                DMA("pool", wv[:, :, 0:256], kchunks(w_fi_d, l, pr2 * 256, 256), [], [bwt])
                DMA("pool", wv[:, :, 256:512], kchunks(w_fi_d, l, DFF + pr2 * 256, 256), [], [bwt])
                for cc in range(2):
                    mm = pr2 * 2 + cc
                    pg_, bpg_ = pnext()
                    for k in range(KC):
                        MM(pg_[:], wv[:, k, cc * 128:(cc + 1) * 128], xn[:, k, :], k == 0, k == KC - 1, [bwt, bxn[k]], [bpg_])
                    pu_, bpu_ = pnext()
                    for k in range(KC):
                        MM(pu_[:], wv[:, k, 256 + cc * 128:256 + (cc + 1) * 128], xn[:, k, :], k == 0, k == KC - 1, [bwt, bxn[k]], [bpu_])
                    ACT(t1[:], pg_[:], AF.Silu, [bpg_], [bt1])
                    TTo("dve", reg[:, mm, :], t1[:], pu_[:], ALU.mult, [bt1, bpu_], [breg[mm]])
            wfov = w_fo_d[l].rearrange("(k p) n -> p k n", p=128)
            for m in range(KC if stop >= 7 else 0):
                wt, bwt = wnext()
                wv = wt[:, 0:FC * 128].rearrange("p (k n) -> p k n", k=FC)
                DMA("pool", wv, wfov[:, :, m * 128:(m + 1) * 128], [], [bwt])
                pt, bpt = pnext()
                for k in range(FC):
                    MM(pt[:], wv[:, k, :], reg[:, k, :], k == 0, k == FC - 1, [bwt, breg[k]], [bpt])
                TTo("dve", Xt[:, m, :], Xt[:, m, :], pt[:], ALU.add, [bX[m], bpt], [bX[m]])
            if last:
                pt, bpt = pnext()
                for k in range(KC):
                    i = k % 2
                    ACT(nsq[i][:], Xt[:, k, :], AF.Square, [bX[k]], [bnsq[i]])
                    MM(pt[:], ones_b[:], nsq[i][:], k == 0, k == KC - 1, [bones, bnsq[i]], [bpt])
                ACT(rstd[:], pt[:], AF.Sqrt, [bpt], [brstd], scale=1.0 / D, bias=EPS)
                S.op("dve", lambda e: e.reciprocal(out=rstd[:], in_=rstd[:]), reads=[brstd], writes=[brstd])
                dstv = yT_d.rearrange("(k p) t -> p k t", p=128)
                for k in range(KC):
                    STT(Xt[:, k, :], Xt[:, k, :], pr[:, P_NF + k:P_NF + k + 1], rstd[:], ALU.mult, ALU.mult,
                        [bX[k], bpr, brstd], [bX[k]])
                    DMA("sp", dstv[:, k, t * TT:(t + 1) * TT], Xt[:, k, :], [bX[k]], [])
            else:
                dstv = xs_d.rearrange("(k p) t -> p k t", p=128)
                for k in range(KC):
                    DMA("sp", dstv[:, k, t * TT:(t + 1) * TT], Xt[:, k, :], [bX[k]], [b_xs[t][k]])
                if t == NTILE - 1:
                    CP("dve", hx[:, 0, :].rearrange("p (k i) -> p k i", k=KC), Xt[:, :, TT - 3:TT], bX, [bhx])
                    DMA("sp", bh_d[:, :], hx[:, 0, :], [bhx], [b_bh])

        def exchange_states(l):
            ACT(smalls[:, 0:32], totacc[:], AF.Exp, [btot], [bsmalls])
            CP("dve", smalls[:, 32:40], hst[:], [bhst], [bsmalls])
            CP("dve", smalls[:, 40:48], rsum[:], [brsum], [bsmalls])
            DMA("sp", bst_d[l][:, 0:2048], Sst[:], bS, [b_bst[l]])
            DMA("sp", bst_d[l][:, 2048:WST], smalls[:], [bsmalls], [b_bsm[l]])
            S.op("pool", lambda e: e.collective_compute("AllGather", ALU.bypass, replica_groups=[list(range(NCORES))],
                                                        ins=[bst_d[l][:, :]], outs=[gst_d[l][:, :]]),
                 reads=[b_bst[l], b_bsm[l]], writes=[b_gst[l]], cc=True)
            gv = gst_d[l].rearrange("(r p) w -> p r w", p=128)
            for i in range(NCORES):
                DMA("sp", gsm[:, i, :], gv[:, i, 2048:WST], [b_gst[l]], [bgsm])
            for g in range(4):
                MSET("dve", Sst[:, g * 512:(g + 1) * 512], 0.0, [bS[g]])
            MSET("dve", hst[:], 0.0, [bhst])
            gS = [y_b[:, 0:8, :].rearrange("p a b -> p (a b)").bitcast(F32),
                  y_b[:, 8:16, :].rearrange("p a b -> p (a b)").bitcast(F32)]
            bgS = [byb[0:8], byb[8:16]]
            for i in range(NCORES - 1):
                mi = msk[:, i:i + 1]
                j = i % 2
                DMA("sp", gS[j], gv[:, i, 0:2048], [b_gst[l]], bgS[j])
                TS("dve", coef[:], gsm[:, i, 0:32], 1.0, mi, ALU.subtract, ALU.mult, [bgsm, bmsk], [bcoef])
                TS("dve", coef[:], coef[:], 1.0, None, ALU.add, None, [bcoef], [bcoef])
                for g in range(4):
                    Sg = Sst[:, g * 512:(g + 1) * 512]
                    TTo("dve", h8(Stmp[:]), h8(Sg), bc8(coef[:, g * 8:(g + 1) * 8]), ALU.mult, [bS[g], bcoef], [bStmp])
                    STT(Sg, gS[j][:, g * 512:(g + 1) * 512], mi, Stmp[:], ALU.mult, ALU.add, bgS[j] + [bmsk, bStmp], [bS[g]])
                TTo("dve", ptmp[:, 0:8], gsm[:, i, 40:48], kk[:], ALU.mult, [bgsm, bkk], [bptmp])
                ACT(ptmp[:, 0:8], ptmp[:, 0:8], AF.Exp, [bptmp], [bptmp])
                TS("dve", ptmp[:, 0:8], ptmp[:, 0:8], 1.0, mi, ALU.subtract, ALU.mult, [bptmp, bmsk], [bptmp])
                TS("dve", ptmp[:, 0:8], ptmp[:, 0:8], 1.0, None, ALU.add, None, [bptmp], [bptmp])
                TTo("dve", ptmp[:, 8:16], hst[:], ptmp[:, 0:8], ALU.mult, [bhst, bptmp], [bptmp])
                STT(hst[:], gsm[:, i, 32:40], mi, ptmp[:, 8:16], ALU.mult, ALU.add, [bgsm, bmsk, bptmp], [bhst])
            for g in range(4):
                ACT(Sbf[:, g * 512:(g + 1) * 512], Sst[:, g * 512:(g + 1) * 512], AF.Copy, [bS[g]], [bSbf[g]])

        def exchange_halo():
            S.op("pool", lambda e: e.collective_compute("AllGather", ALU.bypass, replica_groups=[list(range(NCORES))],
                                                        ins=[bh_d[:, :]], outs=[gh_d[:, :]]),
                 reads=[b_bh], writes=[b_gh], cc=True)
            ghv = gh_d.rearrange("(r p) w -> p r w", p=128)
            for i in range(NCORES):
                DMA("sp", hx[:, i, :], ghv[:, i, :], [b_gh], [bhx])
            xh2 = Xh[:, :, 0:3]
            MSET("dve", Xh[:], 0.0, [bXh])
            for i in range(NCORES - 1):
                STT(xh2, hx[:, i, :].rearrange("p (k i) -> p k i", k=KC), msk[:, 8 + i:9 + i], xh2, ALU.mult, ALU.add,
                    [bhx, bmsk, bXh], [bXh])

        def zero_states():
            for g in range(4):
                MSET("dve", Sst[:, g * 512:(g + 1) * 512], 0.0, [bS[g]])
                MSET("dve", Sbf[:, g * 512:(g + 1) * 512], 0.0, [bSbf[g]])
            MSET("dve", hst[:], 0.0, [bhst])
            MSET("dve", rsum[:], 0.0, [brsum])
            MSET("dve", totacc[:], 0.0, [btot])

        ntl = (dbg or {}).get("ntile", NTILE)
        for l in range(L):
            layer_setup(l)
            zero_states()
            if exchange:
                for t in range(ntl):
                    run_tile(l, t, False, False)
                exchange_states(l)
            for t in range(ntl):
                run_tile(l, t, True, l == L - 1)
            if exchange and l < L - 1:
                exchange_halo()
        S.emit()
    return nc


def _fm(v):
    v = np.asarray(v, np.float32)
    return np.ascontiguousarray(v.reshape(-1, 128).T)


def _pack_params(l, inp):
    pr = np.zeros((128, NPAR), np.float32)
    pr[:, P_N1G:P_N1G + 8] = _fm(inp["norm1_g"][l])
    pr[:, P_N2G:P_N2G + 8] = _fm(inp["norm2_g"][l])
    pr[:, P_BG:P_BG + 16] = _fm(inp["b_gate"][l])
    for k in range(4):
        pr[:, P_LCW + k * 8:P_LCW + k * 8 + 8] = _fm(inp["lru_conv_w"][l, k])
        pr[:, P_SCW + k * 24:P_SCW + k * 24 + 24] = _fm(inp["ssd_conv_w"][l, k])
    pr[:, P_LCB:P_LCB + 8] = _fm(inp["lru_conv_b"][l])
    pr[:, P_LBA:P_LBA + 8] = _fm(inp["lru_b_a"][l])
    pr[:, P_LBX:P_LBX + 8] = _fm(inp["lru_b_x"][l])
    pr[:, P_LAM:P_LAM + 8] = _fm(inp["lru_lambda"][l])
    pr[:, P_SCB:P_SCB + 24] = _fm(inp["ssd_conv_b"][l])
    pr[:, P_SNG:P_SNG + 16] = _fm(inp["ssd_norm_g"][l])
    pr[:, P_NF:P_NF + 8] = _fm(inp["norm_f"])
    pr[:, P_DTB:P_DTB + 32] = np.asarray(inp["ssd_dt_bias"][l], np.float32)[None, :]
    pr[:, P_ALOG:P_ALOG + 32] = np.asarray(inp["ssd_A_log"][l], np.float32)[None, :]
    pr[:, P_D:P_D + 32] = np.asarray(inp["ssd_D"][l], np.float32)[None, :]
    return pr


def _blockdiag(w):
    out = np.zeros((8, 128, 128), np.float32)
    for h in range(16):
        c, o = h // 2, (h % 2) * 64
        out[c, o:o + 64, o:o + 64] = w[h]
    return out


def make_in_maps(inp):
    inp = {k: np.asarray(v, np.float32) for k, v in inp.items()}
    x = inp["x"]
    pr = np.stack([_pack_params(l, inp) for l in range(2)])
    cst = np.zeros((128, 384), np.float32)
    cst[:, 0:128] = np.triu(np.ones((128, 128), np.float32))
    cst[:, 128:256] = np.tril(np.ones((128, 128), np.float32), -1)
    cst[:, 256:384] = np.eye(128, dtype=np.float32)
    wbd_a = np.stack([_blockdiag(inp["lru_w_a"][l]) for l in range(2)])
    wbd_x = np.stack([_blockdiag(inp["lru_w_x"][l]) for l in range(2)])
    shared = dict(pr=pr, cst=cst, w_in=inp["w_in"], wbd_a=wbd_a, wbd_x=wbd_x, w_branch=inp["w_branch"],
                  w_out=inp["w_out"], w_ffn_in=inp["w_ffn_in"], w_ffn_out=inp["w_ffn_out"])
    maps = []
    for c in range(NCORES):
        b, j = c // 4, c % 4
        t0 = j * TSEG
        xT = np.ascontiguousarray(x[b, t0:t0 + TSEG, :].T)
        xh = np.zeros((128, 24), np.float32)
        if j > 0:
            hv = x[b, t0 - 3:t0, :]
            xh[:] = hv.reshape(3, 8, 128).transpose(2, 1, 0).reshape(128, 24)
        msk = np.zeros((128, 16), np.float32)
        for i in range(NCORES):
            if i // 4 == b and i < c:
                msk[:, i] = 1.0
            if j > 0 and i == c - 1:
                msk[:, 8 + i] = 1.0
        m = dict(shared)
        m.update(xT=xT, xh=xh, msk=msk)
        maps.append(m)
    return maps


_NC_CACHE = {}


def kernel(**inputs):
    if "nc" not in _NC_CACHE:
        _NC_CACHE["nc"] = build_program(2, True)
    nc = _NC_CACHE["nc"]
    maps = make_in_maps(inputs)
    res = run_bass_kernel_spmd(nc, maps, core_ids=list(range(NCORES)))
    out = np.empty((2, SEQ, D), np.float32)
    for c in range(NCORES):
        b, j = c // 4, c % 4
        out[b, j * TSEG:(j + 1) * TSEG, :] = res.results[c]["yT"].T
    return out
```

```python
import contextlib
import numpy as np
import concourse.bass as bass
import concourse.mybir as mybir
from concourse.bass_utils import run_bass_kernel_spmd

F32 = mybir.dt.float32
BF16 = mybir.dt.bfloat16
AF = mybir.ActivationFunctionType
ALU = mybir.AluOpType

NCORES = 8
D = 1024
KC = 8
SEQ = 8192
TSEG = 2048
TT = 512
NTILE = TSEG // TT
Q = 128
NQ = TT // Q
IN_DIM = 9248
DFF = 2816
FC = DFF // 128
OFF_LX, OFF_LG, OFF_Z, OFF_XBC, OFF_DT, OFF_G = 0, 1024, 2048, 4096, 7168, 7200
EPS = 1e-6
NPAR = 336
P_N1G, P_N2G, P_BG, P_LCW, P_LCB, P_LBA, P_LBX, P_LAM, P_SCW, P_SCB, P_SNG, P_NF, P_DTB, P_ALOG, P_D = (
    0, 8, 16, 32, 64, 72, 80, 88, 96, 192, 216, 232, 240, 272, 304)
WST = 2048 + 48

SAME_ENGINE_SYNC = True
WSPLIT = 8
N_DMA_SEMS = 8


class Buf:
    __slots__ = ("name", "w", "r", "rd")

    def __init__(self, name):
        self.name = name
        self.w = None
        self.r = {}
        self.rd = []


class Op:
    __slots__ = ("eng", "fn", "deps", "is_dma", "semi", "semv", "needed", "inc")

    def __init__(self, eng, fn, is_dma):
        self.eng = eng
        self.fn = fn
        self.deps = []
        self.is_dma = is_dma
        self.semi = None
        self.semv = None
        self.needed = False
        self.inc = 16


class Sched:
    ENGS = ("pe", "act", "dve", "pool", "sp")

    def __init__(self, nc):
        self.nc = nc
        self.ops = {e: [] for e in self.ENGS}
        self.dma_hist = {e: [] for e in self.ENGS}
        self.cc_count = 0

    def op(self, eng, fn, reads=(), writes=(), dma=False, cc=False):
        o = Op(eng, fn, dma or cc)
        deps = []
        for b in reads:
            if b.w is not None:
                deps.append((b.w, True))
        for b in writes:
            if b.w is not None:
                deps.append((b.w, False))
            deps.extend((x, False) for x in b.r.values())
            deps.extend((x, False) for x in b.rd)
        if cc:
            self.cc_count += 1
            o.semi = "cc"
            o.semv = self.cc_count
            o.inc = 1
        elif dma:
            h = self.dma_hist[eng]
            k = len(h)
            o.semi = k % N_DMA_SEMS
            o.semv = 16 * (k // N_DMA_SEMS + 1)
            if k >= N_DMA_SEMS:
                deps.append((h[k - N_DMA_SEMS], True))
            h.append(o)
        seen = set()
        for d, raw in deps:
            if d is o or id(d) in seen:
                continue
            if (not d.is_dma) and (not o.is_dma) and d.eng == eng:
                if eng == "pe" or not SAME_ENGINE_SYNC:
                    continue
            seen.add(id(d))
            o.deps.append(d)
            d.needed = True
        for b in reads:
            if o.is_dma:
                b.rd.append(o)
            else:
                b.r[eng] = o
        for b in writes:
            b.w = o
            b.r = {}
            b.rd = []
        self.ops[eng].append(o)
        return o

    def emit(self):
        nc = self.nc
        for e in self.ENGS:
            c = 0
            for o in self.ops[e]:
                if o.is_dma:
                    continue
                if o.needed:
                    c += 1
                    o.semv = c
        with contextlib.ExitStack() as st:
            csem = {e: st.enter_context(nc.semaphore("c_" + e)) for e in ("pe", "act", "dve", "pool")}
            dsem = {e: [st.enter_context(nc.semaphore("d_%s%d" % (e, i))) for i in range(N_DMA_SEMS)]
                    for e in ("sp", "pool")}
            ccsem = st.enter_context(nc.semaphore("ccs"))
            block = st.enter_context(nc.Block())

            def semof(d):
                if d.is_dma:
                    return ccsem if d.semi == "cc" else dsem[d.eng][d.semi]
                return csem[d.eng]

            def run(e, handle):
                waited = {}
                last = {}
                for o in self.ops[e]:
                    for d in o.deps:
                        s = semof(d)
                        if waited.get(s.num, 0) >= d.semv:
                            continue
                        waited[s.num] = d.semv
                        handle.wait_ge(s, d.semv)
                    ins = o.fn(handle)
                    if o.is_dma:
                        s = semof(o)
                        ins.then_inc(s, o.inc)
                        last[s.num] = (s, o.semv)
                    elif o.needed:
                        ins.then_inc(csem[e], 1)
                for s, v in last.values():
                    if waited.get(s.num, 0) < v:
                        handle.wait_ge(s, v)

            @block.tensor
            def _(h):
                run("pe", h)

            @block.scalar
            def _(h):
                run("act", h)

            @block.vector
            def _(h):
                run("dve", h)

            @block.gpsimd
            def _(h):
                run("pool", h)

            @block.sync
            def _(h):
                run("sp", h)


def build_program(nlayers=2, exchange=True, dbg=None):
    nc = bass.Bass("TRN2", target_bir_lowering=False)
    S = Sched(nc)
    L = nlayers

    def din(name, shape):
        return nc.dram_tensor(name, list(shape), F32, kind="ExternalInput").ap()

    xT_d = din("xT", [D, TSEG])
    xh_d = din("xh", [128, 24])
    pr_d = din("pr", [2, 128, NPAR])
    msk_d = din("msk", [128, 16])
    cst_d = din("cst", [128, 384])
    w_in_d = din("w_in", [2, D, IN_DIM])
    wbda_d = din("wbd_a", [2, KC, 128, 128])
    wbdx_d = din("wbd_x", [2, KC, 128, 128])
    w_br_d = din("w_branch", [2, 3072, D])
    w_out_d = din("w_out", [2, D, D])
    w_fi_d = din("w_ffn_in", [2, D, 2 * DFF])
    w_fo_d = din("w_ffn_out", [2, DFF, D])
    yT_d = nc.dram_tensor("yT", [D, TSEG], F32, kind="ExternalOutput").ap()
    xs_d = nc.dram_tensor("xscr", [D, TSEG], F32).ap()
    bst_d = [nc.dram_tensor("bst%d" % l, [128, WST], F32).ap() for l in range(2)]
    gst_d = [nc.dram_tensor("gst%d" % l, [NCORES * 128, WST], F32).ap() for l in range(2)]
    bh_d = nc.dram_tensor("bh", [128, 24], F32).ap()
    gh_d = nc.dram_tensor("gh", [NCORES * 128, 24], F32).ap()
    b_xs = [[Buf("xscr%d_%d" % (t, k)) for k in range(KC)] for t in range(NTILE)]
    b_bst = [Buf("bst0"), Buf("bst1")]
    b_bsm = [Buf("bsm0"), Buf("bsm1")]
    b_gst = [Buf("gst0"), Buf("gst1")]
    b_bh, b_gh = Buf("bh"), Buf("gh")

    with contextlib.ExitStack() as st:
        def sb(name, shape, dt=F32):
            return st.enter_context(nc.sbuf_tensor("s_" + name, list(shape), dt))

        Xt = sb("Xt", [128, KC, TT]); bX = [Buf("X%d" % k) for k in range(KC)]
        xn = sb("xn", [128, KC, TT], BF16); bxn = [Buf("xn%d" % k) for k in range(KC)]
        Xh = sb("Xh", [128, KC, 3]); bXh = Buf("Xh")
        xnh = sb("xnh", [128, KC, 3], BF16); bxnh = Buf("xnh")
        NW = 3
        wb = [sb("wb%d" % i, [128, 4096], BF16) for i in range(NW)]
        bwb = [Buf("wb%d" % i) for i in range(NW)]
        cin = [sb("cin%d" % i, [128, TT + 4]) for i in range(2)]; bcin = [Buf("cin0"), Buf("cin1")]
        acc = [sb("acc%d" % i, [128, TT]) for i in range(2)]; bacc = [Buf("acc0"), Buf("acc1")]
        Hh = sb("Hh", [128, 32, 4]); bH = [Buf("H%d" % c) for c in range(32)]
        lt = {n: sb("lt_" + n, [128, TT]) for n in ("u", "r", "i", "a", "h", "gg")}
        blt = {n: Buf("lt_" + n) for n in lt}
        ubf = sb("ubf", [128, TT], BF16); bubf = Buf("ubf")
        u1 = sb("lt_u1", [128, TT]); ubf1 = sb("ubf1", [128, TT], BF16)
        uu = [lt["u"], u1]; buu = [blt["u"], Buf("lt_u1")]
        ubf2 = [ubf, ubf1]; bubf2 = [bubf, Buf("ubf1")]
        hst = sb("hst", [128, KC]); bhst = Buf("hst")
        rsum = sb("rsum", [128, KC]); brsum = Buf("rsum")
        rs_t = sb("rs_t", [128, KC]); brs_t = Buf("rs_t")
        y_a = sb("y_a", [128, KC, TT], BF16); bya = [Buf("ya%d" % c) for c in range(KC)]
        reg = sb("reg", [128, 24, TT], BF16); breg = [Buf("reg%d" % c) for c in range(24)]
        y_b = sb("y_b", [128, 16, TT], BF16); byb = [Buf("yb%d" % c) for c in range(16)]
        mg = sb("mg", [128, KC, TT], BF16); bmg = [Buf("mg%d" % c) for c in range(KC)]
        sm_names = ("dtr", "e1", "dt", "av", "eacs", "dte", "cd", "w2", "dif")
        tq = {n: sb("tq_" + n, [128, NQ, 32]) for n in sm_names}
        btq = {n: [Buf("tq_%s%d" % (n, q)) for q in range(NQ)] for n in sm_names}
        totacc = sb("totacc", [128, 32]); btot = Buf("totacc")
        def dbl(name, shape, dt):
            return [sb("%s%d" % (name, i), shape, dt) for i in range(2)], [Buf("%s%d" % (name, i)) for i in range(2)]
        xsT, bxsT = dbl("xsT", [128, 512], BF16)
        xdt, bxdt = dbl("xdt", [128, 512], BF16)
        xdd, bxdd = dbl("xdd", [128, 512], BF16)
        xsD, bxsD = dbl("xsD", [128, 512], BF16)
        BT, bBT = dbl("BT", [128, 128], BF16)
        Rt, bRt = dbl("Rt", [128, 512], F32)
        Et, bEt = dbl("Et", [128, 512], BF16)
        Lm, bLm = dbl("Lm", [128, 512], BF16)
        smk = sb("smk", [128, 128]); bsmk = Buf("smk")
        smk1 = sb("smk1", [128, 128])
        smk2 = [smk, smk1]; bsmk2 = [bsmk, Buf("smk1")]
        LmB, bLmB = dbl("LmB", [128, 512], BF16)
        Lm4 = [[Lm[0], Lm[1]], [LmB[0], LmB[1]]]; bLm4 = [[bLm[0], bLm[1]], [bLmB[0], bLmB[1]]]
        yo = sb("yo", [128, 512]); byo = Buf("yo")
        yy = sb("yy", [128, 512]); byy = Buf("yy")
        sz = sb("sz", [128, 512]); bsz = Buf("sz")
        sz1 = sb("sz1", [128, 512]); sz0 = sb("sz0", [128, 512])
        sz2 = [sz0, sz1]; bsz2 = [Buf("sz0"), Buf("sz1")]
        ynb = sb("ynb", [128, 512], BF16); bynb = Buf("ynb")
        ynb1 = sb("ynb1", [128, 512], BF16)
        ynb2 = [ynb, ynb1]; bynb2 = [bynb, Buf("ynb1")]
        sq4 = sb("sq4", [128, 4]); bsq4 = Buf("sq4")
        Sst = sb("Sst", [128, 2048]); bS = [Buf("S%d" % g) for g in range(4)]
        Sbf = sb("Sbf", [128, 2048], BF16); bSbf = [Buf("Sbf%d" % g) for g in range(4)]
        Stmp = sb("Stmp", [128, 512]); bStmp = Buf("Stmp")
        ga = sb("ga", [128, TT]); bga = Buf("ga")
        gb = sb("gb", [128, TT]); bgb = Buf("gb")
        t1 = sb("t1", [128, TT]); bt1 = Buf("t1")
        t2 = sb("t2", [128, TT]); bt2 = Buf("t2")
        nsq = [sb("nsq%d" % i, [128, TT], BF16) for i in range(2)]; bnsq = [Buf("nsq0"), Buf("nsq1")]
        rstd = sb("rstd", [128, TT]); brstd = Buf("rstd")
        cst = sb("cst", [128, 384]); bcst = Buf("cst")
        U_f = cst[:, 0:128]
        Ls_f = cst[:, 128:256]
        id_f = cst[:, 256:384]
        ident = sb("ident", [128, 128], BF16); bident = Buf("ident")
        ones_b = sb("ones_b", [128, 128], BF16); bones = Buf("ones")
        ones_f = sb("ones_f", [128, 128]); bonesf = Buf("onesf")
        msk = sb("msk", [128, 16]); bmsk = Buf("msk")
        pr = sb("pr", [128, NPAR]); bpr = Buf("pr")
        kk = sb("kk", [128, KC]); bkk = Buf("kk")
        Ab = sb("Ab", [128, 32]); bAb = Buf("Ab")
        ptmp = sb("ptmp", [128, 32]); bptmp = Buf("ptmp")
        bda = sb("bda", [128, KC, 128], BF16); bbda = Buf("bda")
        bdx = sb("bdx", [128, KC, 128], BF16); bbdx = Buf("bdx")
        wdt = sb("wdt", [128, KC, 32], BF16); bwdt = Buf("wdt")
        smalls = sb("smalls", [128, 48]); bsmalls = Buf("smalls")
        gsm = sb("gsm", [128, NCORES, 48]); bgsm = Buf("gsm")
        coef = sb("coef", [128, 32]); bcoef = Buf("coef")
        hx = sb("hx", [128, NCORES, 24]); bhx = Buf("hx")

        pbank = [st.enter_context(nc.psum_tensor("pb%d" % i, [128, 512], F32)) for i in range(8)]
        bpb = [Buf("pb%d" % i) for i in range(8)]
        pcount = [0]

        def pnext():
            i = pcount[0] % 7
            pcount[0] += 1
            return pbank[i], bpb[i]
        ph, bph = pbank[7], bpb[7]

        wcount = [0]

        def wnext():
            i = wcount[0] % NW
            wcount[0] += 1
            return wb[i], bwb[i]

        def MM(out, lhsT, rhs, start, stop, r, w):
            S.op("pe", lambda e: e.matmul(out, lhsT=lhsT, rhs=rhs, start=start, stop=stop), reads=r, writes=w)

        def TR(out, in_, r, w):
            S.op("pe", lambda e: e.transpose(out=out, in_=in_, identity=ident[:]), reads=r + [bident], writes=w)

        def ACT(out, in_, func, r, w, bias=None, scale=None, accum=None):
            kw = {}
            if bias is not None:
                kw["bias"] = bias
            if scale is not None:
                kw["scale"] = scale
            if accum is not None:
                kw["accum_out"] = accum
            S.op("act", lambda e: e.activation(out=out, in_=in_, func=func, **kw), reads=r, writes=w)

        def TTo(eng, out, in0, in1, op, r, w):
            S.op(eng, lambda e: e.tensor_tensor(out=out, in0=in0, in1=in1, op=op), reads=r, writes=w)

        def TS(eng, out, in0, s1, s2, op0, op1, r, w):
            if s2 is None:
                S.op(eng, lambda e: e.tensor_scalar(out=out, in0=in0, scalar1=s1, scalar2=None, op0=op0), reads=r, writes=w)
            else:
                S.op(eng, lambda e: e.tensor_scalar(out=out, in0=in0, scalar1=s1, scalar2=s2, op0=op0, op1=op1), reads=r, writes=w)

        def STT(out, in0, scalar, in1, op0, op1, r, w):
            S.op("dve", lambda e: e.scalar_tensor_tensor(out=out, in0=in0, scalar=scalar, in1=in1, op0=op0, op1=op1),
                 reads=r, writes=w)

        def CP(eng, out, in_, r, w):
            S.op(eng, lambda e: e.tensor_copy(out=out, in_=in_), reads=r, writes=w)

        def MSET(eng, ap, val, w):
            S.op(eng, lambda e: e.memset(ap, val), writes=w)

        def DMA(q, out, in_, r, w):
            if q == "pool" and len(out.shape) == 3 and WSPLIT:
                nk = out.shape[1]
                for k0 in range(0, nk, WSPLIT):
                    k1 = min(nk, k0 + WSPLIT)
                    S.op(q, lambda e, k0=k0, k1=k1: e.dma_start(out=out[:, k0:k1, :], in_=in_[:, k0:k1, :]),
                         reads=r, writes=w, dma=True)
                return
            S.op(q, lambda e: e.dma_start(out=out, in_=in_), reads=r, writes=w, dma=True)

        def h8(ap):
            return ap.rearrange("p (h d) -> p h d", h=8)

        def bc8(ap8):
            return ap8.unsqueeze(2).to_broadcast([128, 8, 64])

        DMA("sp", cst[:], cst_d[:, :], [], [bcst])
        DMA("sp", msk[:], msk_d[:, :], [], [bmsk])
        CP("dve", ident[:], id_f, [bcst], [bident])
        MSET("dve", ones_b[:], 1.0, [bones])
        MSET("dve", ones_f[:], 1.0, [bonesf])
        DMA("sp", Xh[:], xh_d.rearrange("p (k i) -> p k i", k=KC), [], [bXh])

        def kchunks(wd, l, c0, ncols):
            return wd[l].rearrange("(k p) n -> p k n", p=128)[:, :, c0:c0 + ncols]

        def rmsnorm(src, bsrc, width, gcol, dst, bdst):
            pt, bpt = pnext()
            for k in range(KC):
                i = k % 2
                ACT(nsq[i][:, 0:width], src[:, k, 0:width], AF.Square, [bsrc[k] if isinstance(bsrc, list) else bsrc], [bnsq[i]])
                MM(pt[:, 0:width], ones_b[:], nsq[i][:, 0:width], k == 0, k == KC - 1, [bones, bnsq[i]], [bpt])
            ACT(rstd[:, 0:width], pt[:, 0:width], AF.Sqrt, [bpt], [brstd], scale=1.0 / D, bias=EPS)
            S.op("dve", lambda e: e.reciprocal(out=rstd[:, 0:width], in_=rstd[:, 0:width]), reads=[brstd], writes=[brstd])
            for k in range(KC):
                STT(dst[:, k, 0:width], src[:, k, 0:width], pr[:, gcol + k:gcol + k + 1], rstd[:, 0:width],
                    ALU.mult, ALU.mult,
                    [bsrc[k] if isinstance(bsrc, list) else bsrc, bpr, brstd],
                    [bdst[k] if isinstance(bdst, list) else bdst])

        def layer_setup(l):
            DMA("sp", pr[:], pr_d[l], [], [bpr])
            ACT(kk[:], pr[:, P_LAM:P_LAM + 8], AF.Exp, [bpr], [bkk], scale=-1.0)
            ACT(kk[:], kk[:], AF.Ln, [bkk], [bkk], bias=1.0)
            TS("dve", kk[:], kk[:], -8.0, None, ALU.mult, None, [bkk], [bkk])
            ACT(Ab[:], pr[:, P_ALOG:P_ALOG + 32], AF.Exp, [bpr], [bAb])
            TS("dve", Ab[:], Ab[:], -1.0, None, ALU.mult, None, [bAb], [bAb])
            DMA("pool", bda[:], wbda_d[l].rearrange("c i j -> i c j"), [], [bbda])
            DMA("pool", bdx[:], wbdx_d[l].rearrange("c i j -> i c j"), [], [bbdx])
            DMA("pool", wdt[:], kchunks(w_in_d, l, OFF_DT, 32), [], [bwdt])

        def conv_chunk(l, t, ci, wt, bwt, wcol, post):
            wv = wt[:, 0:4096].rearrange("p (k n) -> p k n", k=KC)
            pt, bpt = pnext()
            for k in range(KC):
                MM(pt[:], wv[:, k, wcol:wcol + 128], xn[:, k, :], k == 0, k == KC - 1, [bwt, bxn[k]], [bpt])
            i = ci % 2
            if t == 0:
                for k in range(KC):
                    MM(ph[:, ci * 4:ci * 4 + 3], wv[:, k, wcol:wcol + 128], xnh[:, k, 0:3], k == 0, k == KC - 1,
                       [bwt, bxnh], [bph])
                ACT(cin[i][:, 0:3], ph[:, ci * 4:ci * 4 + 3], AF.Copy, [bph], [bcin[i]])
            else:
                CP("dve", cin[i][:, 0:3], Hh[:, ci, 0:3], [bH[ci]], [bcin[i]])
            ACT(cin[i][:, 3:3 + TT], pt[:], AF.Copy, [bpt], [bcin[i]])
            CP("dve", Hh[:, ci, 0:3], cin[i][:, TT:TT + 3], [bcin[i]], [bH[ci]])
            if ci < 8:
                wc = lambda kk_: pr[:, P_LCW + kk_ * 8 + ci:P_LCW + kk_ * 8 + ci + 1]
            else:
                wc = lambda kk_: pr[:, P_SCW + kk_ * 24 + (ci - 8):P_SCW + kk_ * 24 + (ci - 8) + 1]
            TS("dve", acc[i][:], cin[i][:, 0:TT], wc(0), None, ALU.mult, None, [bcin[i], bpr], [bacc[i]])
            for kk_ in range(1, 4):
                STT(acc[i][:], cin[i][:, kk_:kk_ + TT], wc(kk_), acc[i][:], ALU.mult, ALU.add,
                    [bcin[i], bpr, bacc[i]], [bacc[i]])
            post(acc[i], bacc[i])

        def run_tile(l, t, full, last):
            src = xT_d if l == 0 else xs_d
            srcv = src.rearrange("(k p) t -> p k t", p=128)
            for k in range(KC):
                DMA("sp", Xt[:, k, :], srcv[:, k, t * TT:(t + 1) * TT], [b_xs[t][k]], [bX[k]])
            rmsnorm(Xt, bX, TT, P_N1G, xn, bxn)
            if t == 0:
                rmsnorm(Xh, bXh, 3, P_N1G, xnh, bxnh)

            stop = (dbg or {}).get("stop", 99)
            lru_w, lru_wg = {}, {}

            def lru_A(c):
                grp, cc = divmod(c, 4)
                if cc == 0:
                    wt, bwt = wnext()
                    DMA("pool", wt[:, 0:4096].rearrange("p (k n) -> p k n", k=KC), kchunks(w_in_d, l, OFF_LX + grp * 512, 512), [], [bwt])
                    lru_w[grp] = (wt, bwt)
                    if full:
                        wg, bwg = wnext()
                        DMA("pool", wg[:, 0:4096].rearrange("p (k n) -> p k n", k=KC), kchunks(w_in_d, l, OFF_LG + grp * 512, 512), [], [bwg])
                        lru_wg[grp] = (wg, bwg)
                wt, bwt = lru_w[grp]
                jb = c % 2

                def post(a_, ba_):
                    ACT(uu[jb][:], a_[:], AF.Identity, [ba_, bpr], [buu[jb]], bias=pr[:, P_LCB + c:P_LCB + c + 1])
                    CP("dve", ubf2[jb][:], uu[jb][:], [buu[jb]], [bubf2[jb]])
                conv_chunk(l, t, c, wt, bwt, cc * 128, post)

            def lru_B(c):
                grp, cc = divmod(c, 4)
                jb = c % 2
                pr_, bpr_ = pnext()
                MM(pr_[:], bda[:, c, :], ubf2[jb][:], True, True, [bbda, bubf2[jb]], [bpr_])
                pi_, bpi_ = pnext()
                MM(pi_[:], bdx[:, c, :], ubf2[jb][:], True, True, [bbdx, bubf2[jb]], [bpi_])
                if full:
                    ACT(lt["r"][:], pr_[:], AF.Sigmoid, [bpr_, bpr], [blt["r"]], bias=pr[:, P_LBA + c:P_LBA + c + 1])
                else:
                    ACT(lt["r"][:], pr_[:], AF.Sigmoid, [bpr_, bpr], [blt["r"], brs_t], bias=pr[:, P_LBA + c:P_LBA + c + 1],
                        accum=rs_t[:, c:c + 1])
                ACT(lt["i"][:], pi_[:], AF.Sigmoid, [bpi_, bpr], [blt["i"]], bias=pr[:, P_LBX + c:P_LBX + c + 1])
                ACT(lt["a"][:], lt["r"][:], AF.Exp, [blt["r"], bkk], [blt["a"]], scale=kk[:, c:c + 1])
                TTo("dve", lt["r"][:], lt["a"][:], lt["a"][:], ALU.mult, [blt["a"]], [blt["r"]])
                ACT(lt["r"][:], lt["r"][:], AF.Sqrt, [blt["r"]], [blt["r"]], scale=-1.0, bias=1.0)
                TTo("dve", lt["i"][:], lt["i"][:], uu[jb][:], ALU.mult, [blt["i"], buu[jb]], [blt["i"]])
                TTo("dve", lt["i"][:], lt["i"][:], lt["r"][:], ALU.mult, [blt["i"], blt["r"]], [blt["i"]])
                S.op("dve", lambda e: e.tensor_tensor_scan(out=lt["h"][:], data0=lt["a"][:], data1=lt["i"][:],
                                                          initial=hst[:, c:c + 1], op0=ALU.mult, op1=ALU.add),
                     reads=[blt["a"], blt["i"], bhst], writes=[blt["h"]])
                CP("dve", hst[:, c:c + 1], lt["h"][:, TT - 1:TT], [blt["h"]], [bhst])
                if full:
                    wg, bwg = lru_wg[grp]
                    wgv = wg[:, 0:4096].rearrange("p (k n) -> p k n", k=KC)
                    pg_, bpg_ = pnext()
                    for k in range(KC):
                        MM(pg_[:], wgv[:, k, cc * 128:cc * 128 + 128], xn[:, k, :], k == 0, k == KC - 1, [bwg, bxn[k]], [bpg_])
                    ACT(lt["gg"][:], pg_[:], AF.Gelu_apprx_tanh, [bpg_], [blt["gg"]])
                    TTo("dve", y_a[:, c, :], lt["gg"][:], lt["h"][:], ALU.mult, [blt["gg"], blt["h"]], [bya[c]])

            stop = 99
            lru_A(0)
            for c in range(KC):
                if c + 1 < KC:
                    lru_A(c + 1)
                lru_B(c)
            if not full:
                TTo("dve", rsum[:], rsum[:], rs_t[:], ALU.add, [brsum, brs_t], [brsum])

            nxbc = 24 if full else 20
            for grp in range(nxbc // 4 if stop >= 2 else 0):
                wt, bwt = wnext()
                DMA("pool", wt[:, 0:4096].rearrange("p (k n) -> p k n", k=KC), kchunks(w_in_d, l, OFF_XBC + grp * 512, 512), [], [bwt])
                for cc in range(4):
                    c = grp * 4 + cc

                    def post(a_, ba_, c=c):
                        ACT(reg[:, c, :], a_[:], AF.Silu, [ba_, bpr], [breg[c]], bias=pr[:, P_SCB + c:P_SCB + c + 1])
                    conv_chunk(l, t, 8 + c, wt, bwt, cc * 128, post)

            for q in range(NQ if stop >= 3 else 0):
                qs = slice(q * Q, (q + 1) * Q)
                pt, bpt = pnext()
                for k in range(KC):
                    MM(pt[:, 0:32], xn[:, k, qs], wdt[:, k, :], k == 0, k == KC - 1, [bxn[k], bwdt], [bpt])
                TTo("dve", tq["dtr"][:, q, :], pt[:, 0:32], pr[:, P_DTB:P_DTB + 32], ALU.add, [bpt, bpr], [btq["dtr"][q]])
                ACT(tq["e1"][:, q, :], tq["dtr"][:, q, :], AF.Exp, [btq["dtr"][q]], [btq["e1"][q]])
                ACT(tq["dt"][:, q, :], tq["e1"][:, q, :], AF.Ln, [btq["e1"][q]], [btq["dt"][q]], bias=1.0)
                TTo("dve", tq["av"][:, q, :], tq["dt"][:, q, :], Ab[:], ALU.mult, [btq["dt"][q], bAb], [btq["av"][q]])
                p2, bp2 = pnext()
                MM(p2[:, 0:32], U_f, tq["av"][:, q, :], True, False, [bcst, btq["av"][q]], [bp2])
                MM(p2[:, 32:64], ones_f[:], tq["av"][:, q, :], False, True, [bonesf, btq["av"][q]], [bp2])
                CP("dve", tq["e1"][:, q, :], p2[:, 0:32], [bp2], [btq["e1"][q]])
                CP("dve", tq["dif"][:, q, :], p2[:, 32:64], [bp2], [btq["dif"][q]])
                ACT(tq["eacs"][:, q, :], tq["e1"][:, q, :], AF.Exp, [btq["e1"][q]], [btq["eacs"][q]])
                ACT(tq["cd"][:, q, :], tq["dif"][:, q, :], AF.Exp, [btq["dif"][q]], [btq["cd"][q]])
                if not full:
                    TTo("dve", totacc[:], totacc[:], tq["dif"][:, q, :], ALU.add, [btot, btq["dif"][q]], [btot])
                TTo("dve", tq["dif"][:, q, :], tq["dif"][:, q, :], tq["e1"][:, q, :], ALU.subtract, [btq["dif"][q], btq["e1"][q]], [btq["dif"][q]])
                ACT(tq["dte"][:, q, :], tq["dif"][:, q, :], AF.Exp, [btq["dif"][q]], [btq["dte"][q]])
                TTo("dve", tq["w2"][:, q, :], tq["dt"][:, q, :], tq["dte"][:, q, :], ALU.mult, [btq["dt"][q], btq["dte"][q]], [btq["w2"][q]])

            ssd_wz = {}

            def ssd_S1(g, q, j):
                gh = slice(g * 8, (g + 1) * 8)
                qs = slice(q * Q, (q + 1) * Q)
                if full and q == 0:
                    wz, bwz = wnext()
                    DMA("pool", wz[:, 0:4096].rearrange("p (k n) -> p k n", k=KC), kchunks(w_in_d, l, OFF_Z + g * 512, 512), [], [bwz])
                    ssd_wz[g] = (wz, bwz)
                pt, bpt = pnext()
                ptb = pt[:].bitcast(BF16)
                for i4 in range(4):
                    TR(ptb[:, i4 * 128:(i4 + 1) * 128], reg[:, g * 4 + i4, qs], [breg[g * 4 + i4]], [bpt])
                ACT(xsT[j][:], ptb[:, 0:512], AF.Copy, [bpt], [bxsT[j]])
                TTo("dve", h8(xdd[j][:]), h8(xsT[j][:]), bc8(tq["w2"][:, q, gh]), ALU.mult, [bxsT[j], btq["w2"][q]], [bxdd[j]])
                p3, bp3 = pnext()
                p3b = p3[:].bitcast(BF16)
                TR(p3b[:, 0:128], reg[:, 16 + g, qs], [breg[16 + g]], [bp3])
                ACT(BT[j][:], p3b[:, 0:128], AF.Copy, [bp3], [bBT[j]])
                if not full:
                    return
                TTo("dve", h8(xdt[j][:]), h8(xsT[j][:]), bc8(tq["dt"][:, q, gh]), ALU.mult, [bxsT[j], btq["dt"][q]], [bxdt[j]])
                TTo("dve", h8(xsD[j][:]), h8(xsT[j][:]), bc8(pr[:, P_D + g * 8:P_D + g * 8 + 8]), ALU.mult, [bxsT[j], bpr], [bxsD[j]])
                p4, bp4 = pnext()
                MM(p4[:, 0:128], reg[:, 16 + g, qs], reg[:, 20 + g, qs], True, True, [breg[16 + g], breg[20 + g]], [bp4])
                TTo("dve", smk2[j][:], p4[:, 0:128], U_f, ALU.mult, [bp4, bcst], [bsmk2[j]])
                for half in range(2):
                    hs = slice(g * 8 + half * 4, g * 8 + half * 4 + 4)
                    TTo("dve", Rt[half][:].rearrange("p (h l) -> p h l", h=4),
                        U_f.unsqueeze(1).to_broadcast([128, 4, 128]),
                        tq["av"][:, q, hs].unsqueeze(2).to_broadcast([128, 4, 128]), ALU.mult,
                        [bcst, btq["av"][q]], [bRt[half]])
                    p7, bp7 = pnext()
                    MM(p7[:], Ls_f, Rt[half][:], True, True, [bcst, bRt[half]], [bp7])
                    ACT(Et[half][:], p7[:], AF.Exp, [bp7], [bEt[half]])
                    TTo("dve", Lm4[j][half][:].rearrange("p (h l) -> p h l", h=4), Et[half][:].rearrange("p (h l) -> p h l", h=4),
                        smk2[j][:].unsqueeze(1).to_broadcast([128, 4, 128]), ALU.mult, [bEt[half], bsmk2[j]], [bLm4[j][half]])
                wz, bwz = ssd_wz[g]
                wzv = wz[:, 0:4096].rearrange("p (k n) -> p k n", k=KC)
                p8, bp8 = pnext()
                for k in range(KC):
                    MM(p8[:], xn[:, k, qs], wzv[:, k, :], k == 0, k == KC - 1, [bxn[k], bwz], [bp8])
                ACT(sz2[j][:], p8[:], AF.Silu, [bp8], [bsz2[j]])

            def ssd_S2(g, q, j):
                gh = slice(g * 8, (g + 1) * 8)
                qs = slice(q * Q, (q + 1) * Q)
                Sg = Sst[:, g * 512:(g + 1) * 512]
                Sbg = Sbf[:, g * 512:(g + 1) * 512]
                if full:
                    p5, bp5 = pnext()
                    for hh in range(8):
                        MM(p5[:, hh * 64:(hh + 1) * 64], reg[:, 20 + g, qs], Sbg[:, hh * 64:(hh + 1) * 64], hh == 0, hh == 7,
                           [breg[20 + g], bSbf[g]], [bp5])
                pA, bpA = pnext()
                for hh in range(8):
                    MM(pA[:, hh * 64:(hh + 1) * 64], BT[j][:], xdd[j][:, hh * 64:(hh + 1) * 64], hh == 0, hh == 7,
                       [bBT[j], bxdd[j]], [bpA])
                TTo("dve", h8(Stmp[:]), h8(Sg), bc8(tq["cd"][:, q, gh]), ALU.mult, [bS[g], btq["cd"][q]], [bStmp])
                TTo("dve", Sg, Stmp[:], pA[:], ALU.add, [bStmp, bpA], [bS[g]])
                if not full:
                    return
                ACT(Sbg, Sg, AF.Copy, [bS[g]], [bSbf[g]])
                TTo("dve", h8(yo[:]), h8(p5[:]), bc8(tq["eacs"][:, q, gh]), ALU.mult, [bp5, btq["eacs"][q]], [byo])
                p6, bp6 = pnext()
                MM(p6[:], ident[:], xsD[j][:], True, False, [bident, bxsD[j]], [bp6])
                for half in range(2):
                    for h4 in range(4):
                        hh = half * 4 + h4
                        MM(p6[:, hh * 64:(hh + 1) * 64], Lm4[j][half][:, h4 * 128:(h4 + 1) * 128], xdt[j][:, hh * 64:(hh + 1) * 64],
                           False, (hh == 7), [bLm4[j][half], bxdt[j]], [bp6])
                TTo("dve", yy[:], p6[:], yo[:], ALU.add, [bp6, byo], [byy])
                TTo("dve", yy[:], yy[:], sz2[j][:], ALU.mult, [byy, bsz2[j]], [byy])
                ACT(sz[:], yy[:], AF.Square, [byy], [bsz, bsq4], accum=sq4[:, 0:1])
                ACT(sq4[:, 1:2], sq4[:, 0:1], AF.Sqrt, [bsq4], [bsq4], scale=1.0 / 512.0, bias=EPS)
                S.op("dve", lambda e: e.reciprocal(out=sq4[:, 2:3], in_=sq4[:, 1:2]), reads=[bsq4], writes=[bsq4])
                TS("dve", ynb2[j][:], yy[:], sq4[:, 2:3], None, ALU.mult, None, [byy, bsq4], [bynb2[j]])

            def ssd_S3(g, q, j):
                qs = slice(q * Q, (q + 1) * Q)
                p9, bp9 = pnext()
                p9b = p9[:].bitcast(BF16)
                for i4 in range(4):
                    TR(p9b[:, i4 * 128:(i4 + 1) * 128], ynb2[j][:, i4 * 128:(i4 + 1) * 128], [bynb2[j]], [bp9])
                for i4 in range(4):
                    cch = g * 4 + i4
                    ACT(y_b[:, cch, qs], p9b[:, i4 * 128:(i4 + 1) * 128], AF.Copy, [bp9, bpr], [byb[cch]],
                        scale=pr[:, P_SNG + cch:P_SNG + cch + 1])

            iters = [(g, q) for g in range(4) for q in range(NQ)]
            ssd_S1(iters[0][0], iters[0][1], 0)
            for n, (g, q) in enumerate(iters):
                if n + 1 < len(iters):
                    ssd_S1(iters[n + 1][0], iters[n + 1][1], (n + 1) % 2)
                ssd_S2(g, q, n % 2)
                if full and n >= 1:
                    ssd_S3(iters[n - 1][0], iters[n - 1][1], (n - 1) % 2)
            if full:
                ssd_S3(iters[-1][0], iters[-1][1], (len(iters) - 1) % 2)
            if not full:
                return

            wbrv = w_br_d[l].rearrange("(k p) n -> p k n", p=128)
            for m in range(KC if stop >= 5 else 0):
                wgt, bwgt = wnext()
                wgv2 = wgt[:, 0:2048].rearrange("p (k n) -> p k n", k=KC)
                DMA("pool", wgv2[:, :, 0:128], kchunks(w_in_d, l, OFF_G + m * 128, 128), [], [bwgt])
                DMA("pool", wgv2[:, :, 128:256], kchunks(w_in_d, l, OFF_G + 1024 + m * 128, 128), [], [bwgt])
                wbt, bwbt = wnext()
                wbv = wbt[:, 0:3072].rearrange("p (k n) -> p k n", k=24)
                DMA("pool", wbv, wbrv[:, :, m * 128:(m + 1) * 128], [], [bwbt])
                pa_, bpa_ = pnext()
                for k in range(KC):
                    MM(pa_[:], wgv2[:, k, 0:128], xn[:, k, :], k == 0, k == KC - 1, [bwgt, bxn[k]], [bpa_])
                ACT(ga[:], pa_[:], AF.Sigmoid, [bpa_, bpr], [bga], bias=pr[:, P_BG + m:P_BG + m + 1])
                pb_, bpb_ = pnext()
                for k in range(KC):
                    MM(pb_[:], wgv2[:, k, 128:256], xn[:, k, :], k == 0, k == KC - 1, [bwgt, bxn[k]], [bpb_])
                ACT(gb[:], pb_[:], AF.Sigmoid, [bpb_, bpr], [bgb], bias=pr[:, P_BG + 8 + m:P_BG + 8 + m + 1])
                pc_, bpc_ = pnext()
                for k in range(KC):
                    MM(pc_[:], wbv[:, k, :], y_a[:, k, :], k == 0, k == KC - 1, [bwbt, bya[k]], [bpc_])
                TTo("dve", t1[:], pc_[:], ga[:], ALU.mult, [bpc_, bga], [bt1])
                pd_, bpd_ = pnext()
                for k in range(16):
                    MM(pd_[:], wbv[:, 8 + k, :], y_b[:, k, :], k == 0, k == 15, [bwbt, byb[k]], [bpd_])
                TTo("dve", t2[:], pd_[:], gb[:], ALU.mult, [bpd_, bgb], [bt2])
                TTo("dve", mg[:, m, :], t1[:], t2[:], ALU.add, [bt1, bt2], [bmg[m]])
            for grp in range(2 if stop >= 6 else 0):
                wt, bwt = wnext()
                wv = wt[:, 0:4096].rearrange("p (k n) -> p k n", k=KC)
                DMA("pool", wv, kchunks(w_out_d, l, grp * 512, 512), [], [bwt])
                for cc in range(4):
                    m = grp * 4 + cc
                    pt, bpt = pnext()
                    for k in range(KC):
                        MM(pt[:], wv[:, k, cc * 128:(cc + 1) * 128], mg[:, k, :], k == 0, k == KC - 1, [bwt, bmg[k]], [bpt])
                    TTo("dve", Xt[:, m, :], Xt[:, m, :], pt[:], ALU.add, [bX[m], bpt], [bX[m]])
            if stop >= 7:
                rmsnorm(Xt, bX, TT, P_N2G, xn, bxn)
            for pr2 in range(FC // 2 if stop >= 7 else 0):
                wt, bwt = wnext()
                wv = wt[:, 0:4096].rearrange("p (k n) -> p k n", k=KC)
                DMA("pool", wv[:, :, 0:256], kchunks(w_fi_d, l, pr2 * 256, 256), [], [bwt])
                DMA("pool", wv[:, :, 256:512], kchunks(w_fi_d, l, DFF + pr2 * 256, 256), [], [bwt])
                for cc in range(2):
                    mm = pr2 * 2 + cc
                    pg_, bpg_ = pnext()
                    for k in range(KC):
                        MM(pg_[:], wv[:, k, cc * 128:(cc + 1) * 128], xn[:, k, :], k == 0, k == KC - 1, [bwt, bxn[k]], [bpg_])
                    pu_, bpu_ = pnext()
                    for k in range(KC):
                        MM(pu_[:], wv[:, k, 256 + cc * 128:256 + (cc + 1) * 128], xn[:, k, :], k == 0, k == KC - 1, [bwt, bxn[k]], [bpu_])
                    ACT(t1[:], pg_[:], AF.Silu, [bpg_], [bt1])
                    TTo("dve", reg[:, mm, :], t1[:], pu_[:], ALU.mult, [bt1, bpu_], [breg[mm]])
            wfov = w_fo_d[l].rearrange("(k p) n -> p k n", p=128)
            for m in range(KC if stop >= 7 else 0):
                wt, bwt = wnext()
                wv = wt[:, 0:FC * 128].rearrange("p (k n) -> p k n", k=FC)
                DMA("pool", wv, wfov[:, :, m * 128:(m + 1) * 128], [], [bwt])
                pt, bpt = pnext()
                for k in range(FC):
                    MM(pt[:], wv[:, k, :], reg[:, k, :], k == 0, k == FC - 1, [bwt, breg[k]], [bpt])
                TTo("dve", Xt[:, m, :], Xt[:, m, :], pt[:], ALU.add, [bX[m], bpt], [bX[m]])
            if last:
                pt, bpt = pnext()
                for k in range(KC):
                    i = k % 2
                    ACT(nsq[i][:], Xt[:, k, :], AF.Square, [bX[k]], [bnsq[i]])
                    MM(pt[:], ones_b[:], nsq[i][:], k == 0, k == KC - 1, [bones, bnsq[i]], [bpt])
                ACT(rstd[:], pt[:], AF.Sqrt, [bpt], [brstd], scale=1.0 / D, bias=EPS)
                S.op("dve", lambda e: e.reciprocal(out=rstd[:], in_=rstd[:]), reads=[brstd], writes=[brstd])
                dstv = yT_d.rearrange("(k p) t -> p k t", p=128)
                for k in range(KC):
                    STT(Xt[:, k, :], Xt[:, k, :], pr[:, P_NF + k:P_NF + k + 1], rstd[:], ALU.mult, ALU.mult,
                        [bX[k], bpr, brstd], [bX[k]])
                    DMA("sp", dstv[:, k, t * TT:(t + 1) * TT], Xt[:, k, :], [bX[k]], [])
            else:
                dstv = xs_d.rearrange("(k p) t -> p k t", p=128)
                for k in range(KC):
                    DMA("sp", dstv[:, k, t * TT:(t + 1) * TT], Xt[:, k, :], [bX[k]], [b_xs[t][k]])
                if t == NTILE - 1:
                    CP("dve", hx[:, 0, :].rearrange("p (k i) -> p k i", k=KC), Xt[:, :, TT - 3:TT], bX, [bhx])
                    DMA("sp", bh_d[:, :], hx[:, 0, :], [bhx], [b_bh])

        def exchange_states(l):
            ACT(smalls[:, 0:32], totacc[:], AF.Exp, [btot], [bsmalls])
            CP("dve", smalls[:, 32:40], hst[:], [bhst], [bsmalls])
            CP("dve", smalls[:, 40:48], rsum[:], [brsum], [bsmalls])
            DMA("sp", bst_d[l][:, 0:2048], Sst[:], bS, [b_bst[l]])
            DMA("sp", bst_d[l][:, 2048:WST], smalls[:], [bsmalls], [b_bsm[l]])
            S.op("pool", lambda e: e.collective_compute("AllGather", ALU.bypass, replica_groups=[list(range(NCORES))],
                                                        ins=[bst_d[l][:, :]], outs=[gst_d[l][:, :]]),
                 reads=[b_bst[l], b_bsm[l]], writes=[b_gst[l]], cc=True)
            gv = gst_d[l].rearrange("(r p) w -> p r w", p=128)
            for i in range(NCORES):
                DMA("sp", gsm[:, i, :], gv[:, i, 2048:WST], [b_gst[l]], [bgsm])
            for g in range(4):
                MSET("dve", Sst[:, g * 512:(g + 1) * 512], 0.0, [bS[g]])
            MSET("dve", hst[:], 0.0, [bhst])
            gS = [y_b[:, 0:8, :].rearrange("p a b -> p (a b)").bitcast(F32),
                  y_b[:, 8:16, :].rearrange("p a b -> p (a b)").bitcast(F32)]
            bgS = [byb[0:8], byb[8:16]]
            for i in range(NCORES - 1):
                mi = msk[:, i:i + 1]
                j = i % 2
                DMA("sp", gS[j], gv[:, i, 0:2048], [b_gst[l]], bgS[j])
                TS("dve", coef[:], gsm[:, i, 0:32], 1.0, mi, ALU.subtract, ALU.mult, [bgsm, bmsk], [bcoef])
                TS("dve", coef[:], coef[:], 1.0, None, ALU.add, None, [bcoef], [bcoef])
                for g in range(4):
                    Sg = Sst[:, g * 512:(g + 1) * 512]
                    TTo("dve", h8(Stmp[:]), h8(Sg), bc8(coef[:, g * 8:(g + 1) * 8]), ALU.mult, [bS[g], bcoef], [bStmp])
                    STT(Sg, gS[j][:, g * 512:(g + 1) * 512], mi, Stmp[:], ALU.mult, ALU.add, bgS[j] + [bmsk, bStmp], [bS[g]])
                TTo("dve", ptmp[:, 0:8], gsm[:, i, 40:48], kk[:], ALU.mult, [bgsm, bkk], [bptmp])
                ACT(ptmp[:, 0:8], ptmp[:, 0:8], AF.Exp, [bptmp], [bptmp])
                TS("dve", ptmp[:, 0:8], ptmp[:, 0:8], 1.0, mi, ALU.subtract, ALU.mult, [bptmp, bmsk], [bptmp])
                TS("dve", ptmp[:, 0:8], ptmp[:, 0:8], 1.0, None, ALU.add, None, [bptmp], [bptmp])
                TTo("dve", ptmp[:, 8:16], hst[:], ptmp[:, 0:8], ALU.mult, [bhst, bptmp], [bptmp])
                STT(hst[:], gsm[:, i, 32:40], mi, ptmp[:, 8:16], ALU.mult, ALU.add, [bgsm, bmsk, bptmp], [bhst])
            for g in range(4):
                ACT(Sbf[:, g * 512:(g + 1) * 512], Sst[:, g * 512:(g + 1) * 512], AF.Copy, [bS[g]], [bSbf[g]])

        def exchange_halo():
            S.op("pool", lambda e: e.collective_compute("AllGather", ALU.bypass, replica_groups=[list(range(NCORES))],
                                                        ins=[bh_d[:, :]], outs=[gh_d[:, :]]),
                 reads=[b_bh], writes=[b_gh], cc=True)
            ghv = gh_d.rearrange("(r p) w -> p r w", p=128)
            for i in range(NCORES):
                DMA("sp", hx[:, i, :], ghv[:, i, :], [b_gh], [bhx])
            xh2 = Xh[:, :, 0:3]
            MSET("dve", Xh[:], 0.0, [bXh])
            for i in range(NCORES - 1):
                STT(xh2, hx[:, i, :].rearrange("p (k i) -> p k i", k=KC), msk[:, 8 + i:9 + i], xh2, ALU.mult, ALU.add,
                    [bhx, bmsk, bXh], [bXh])

        def zero_states():
            for g in range(4):
                MSET("dve", Sst[:, g * 512:(g + 1) * 512], 0.0, [bS[g]])
                MSET("dve", Sbf[:, g * 512:(g + 1) * 512], 0.0, [bSbf[g]])
            MSET("dve", hst[:], 0.0, [bhst])
            MSET("dve", rsum[:], 0.0, [brsum])
            MSET("dve", totacc[:], 0.0, [btot])

        ntl = (dbg or {}).get("ntile", NTILE)
        for l in range(L):
            layer_setup(l)
            zero_states()
            if exchange:
                for t in range(ntl):
                    run_tile(l, t, False, False)
                exchange_states(l)
            for t in range(ntl):
                run_tile(l, t, True, l == L - 1)
            if exchange and l < L - 1:
                exchange_halo()
        S.emit()
    return nc


def _fm(v):
    v = np.asarray(v, np.float32)
    return np.ascontiguousarray(v.reshape(-1, 128).T)


def _pack_params(l, inp):
    pr = np.zeros((128, NPAR), np.float32)
    pr[:, P_N1G:P_N1G + 8] = _fm(inp["norm1_g"][l])
    pr[:, P_N2G:P_N2G + 8] = _fm(inp["norm2_g"][l])
    pr[:, P_BG:P_BG + 16] = _fm(inp["b_gate"][l])
    for k in range(4):
        pr[:, P_LCW + k * 8:P_LCW + k * 8 + 8] = _fm(inp["lru_conv_w"][l, k])
        pr[:, P_SCW + k * 24:P_SCW + k * 24 + 24] = _fm(inp["ssd_conv_w"][l, k])
    pr[:, P_LCB:P_LCB + 8] = _fm(inp["lru_conv_b"][l])
    pr[:, P_LBA:P_LBA + 8] = _fm(inp["lru_b_a"][l])
    pr[:, P_LBX:P_LBX + 8] = _fm(inp["lru_b_x"][l])
    pr[:, P_LAM:P_LAM + 8] = _fm(inp["lru_lambda"][l])
    pr[:, P_SCB:P_SCB + 24] = _fm(inp["ssd_conv_b"][l])
    pr[:, P_SNG:P_SNG + 16] = _fm(inp["ssd_norm_g"][l])
    pr[:, P_NF:P_NF + 8] = _fm(inp["norm_f"])
    pr[:, P_DTB:P_DTB + 32] = np.asarray(inp["ssd_dt_bias"][l], np.float32)[None, :]
    pr[:, P_ALOG:P_ALOG + 32] = np.asarray(inp["ssd_A_log"][l], np.float32)[None, :]
    pr[:, P_D:P_D + 32] = np.asarray(inp["ssd_D"][l], np.float32)[None, :]
    return pr


def _blockdiag(w):
    out = np.zeros((8, 128, 128), np.float32)
    for h in range(16):
        c, o = h // 2, (h % 2) * 64
        out[c, o:o + 64, o:o + 64] = w[h]
    return out


def make_in_maps(inp):
    inp = {k: np.asarray(v, np.float32) for k, v in inp.items()}
    x = inp["x"]
    pr = np.stack([_pack_params(l, inp) for l in range(2)])
    cst = np.zeros((128, 384), np.float32)
    cst[:, 0:128] = np.triu(np.ones((128, 128), np.float32))
    cst[:, 128:256] = np.tril(np.ones((128, 128), np.float32), -1)
    cst[:, 256:384] = np.eye(128, dtype=np.float32)
    wbd_a = np.stack([_blockdiag(inp["lru_w_a"][l]) for l in range(2)])
    wbd_x = np.stack([_blockdiag(inp["lru_w_x"][l]) for l in range(2)])
    shared = dict(pr=pr, cst=cst, w_in=inp["w_in"], wbd_a=wbd_a, wbd_x=wbd_x, w_branch=inp["w_branch"],
                  w_out=inp["w_out"], w_ffn_in=inp["w_ffn_in"], w_ffn_out=inp["w_ffn_out"])
    maps = []
    for c in range(NCORES):
        b, j = c // 4, c % 4
        t0 = j * TSEG
        xT = np.ascontiguousarray(x[b, t0:t0 + TSEG, :].T)
        xh = np.zeros((128, 24), np.float32)
        if j > 0:
            hv = x[b, t0 - 3:t0, :]
            xh[:] = hv.reshape(3, 8, 128).transpose(2, 1, 0).reshape(128, 24)
        msk = np.zeros((128, 16), np.float32)
        for i in range(NCORES):
            if i // 4 == b and i < c:
                msk[:, i] = 1.0
            if j > 0 and i == c - 1:
                msk[:, 8 + i] = 1.0
        m = dict(shared)
        m.update(xT=xT, xh=xh, msk=msk)
        maps.append(m)
    return maps


_NC_CACHE = {}


def kernel(**inputs):
    if "nc" not in _NC_CACHE:
        _NC_CACHE["nc"] = build_program(2, True)
    nc = _NC_CACHE["nc"]
    maps = make_in_maps(inputs)
    res = run_bass_kernel_spmd(nc, maps, core_ids=list(range(NCORES)))
    out = np.empty((2, SEQ, D), np.float32)
    for c in range(NCORES):
        b, j = c // 4, c % 4
        out[b, j * TSEG:(j + 1) * TSEG, :] = res.results[c]["yT"].T
    return out
```
